# Optimizing a Trainium2 kernel written in Bass

```python
import math
import jax, jax.numpy as jnp
from jax import lax
import numpy as np

D_MODEL = 2048
BATCH = 4
SEQ = 4096
DEPTH = 4

N_MIXERS = 3
HEAD_DIM = 128
N_HEADS = D_MODEL // HEAD_DIM
ROPE_THETA = 10000.0
D_FF = 4 * D_MODEL
EPS = 1e-6
SB_Q_BLOCK = 128
S5_GROUP = 16
S5_GROUPS = D_MODEL // S5_GROUP
S5_STATE = 64
S5_CHUNK = 128
S5_DT_MIN = 0.001
S5_DT_MAX = 0.1
IDX_HEADS = 16
IDX_DIM = 64
DSA_TOPK_MAX = 256
DSA_Q_BLOCK = 32
DSA_IN = 3 * D_MODEL + IDX_HEADS * IDX_DIM + IDX_DIM + IDX_HEADS

kernel_name = "hybrid_sb_s5_dsa_trunk"


def rmsnorm(x, g):
    xf = x.astype(jnp.float32)
    y = xf * lax.rsqrt(jnp.mean(xf * xf, axis=-1, keepdims=True) + EPS)
    return (y * g.astype(jnp.float32)).astype(x.dtype)


def rope(x, positions):
    d = x.shape[-1]
    inv_freq = ROPE_THETA ** (-jnp.arange(0, d, 2, dtype=jnp.float32) / d)
    ang = positions.astype(jnp.float32)[..., None] * inv_freq
    cos = jnp.cos(ang)[:, :, None, :]
    sin = jnp.sin(ang)[:, :, None, :]
    xf = x.astype(jnp.float32)
    x1, x2 = xf[..., : d // 2], xf[..., d // 2:]
    return jnp.concatenate([x1 * cos - x2 * sin, x2 * cos + x1 * sin], axis=-1).astype(x.dtype)


def modulate(h, shift, scale):
    return h * (1 + scale[:, None, :]) + shift[:, None, :]


def sq_relu_mlp(h, w1, w2):
    a = jax.nn.relu(h @ w1)
    return (a * a) @ w2


def stick_breaking_mixer(h, w_in, q_gain, k_gain, w_out):
    B, S, _ = h.shape
    q, k, v = jnp.split(h @ w_in, 3, axis=-1)
    q = rmsnorm(q.reshape(B, S, N_HEADS, HEAD_DIM), q_gain)
    k = rmsnorm(k.reshape(B, S, N_HEADS, HEAD_DIM), k_gain)
    v = v.reshape(B, S, N_HEADS, HEAD_DIM)
    nb = S // SB_Q_BLOCK
    qb = q.reshape(B, nb, SB_Q_BLOCK, N_HEADS, HEAD_DIM).transpose(1, 0, 3, 2, 4)
    kpos = jnp.arange(S)
    scale = HEAD_DIM ** -0.5

    def block(args):
        qblk, start = args
        qpos = start + jnp.arange(SB_Q_BLOCK)
        z = jnp.einsum('bhtd,bshd->bhts', qblk, k, preferred_element_type=jnp.float32) * scale
        causal = kpos[None, :] < qpos[:, None]
        log_not = jnp.where(causal, jax.nn.log_sigmoid(-z), 0.0)
        later = lax.cumsum(log_not, axis=3, reverse=True) - log_not
        w = jnp.where(causal, jnp.exp(jax.nn.log_sigmoid(z) + later), 0.0)
        return jnp.einsum('bhts,bshd->bthd', w.astype(v.dtype), v)

    starts = jnp.arange(nb) * SB_Q_BLOCK
    o = lax.map(block, (qb, starts))
    o = o.transpose(1, 0, 2, 3, 4).reshape(B, S, D_MODEL)
    return o @ w_out


def s5_mixer(h, w_in, lam_re, lam_im, log_dt, b_re, b_im, c_re, c_im, d_skip, w_glu):
    B, S, _ = h.shape
    u = h @ w_in
    ug = u.astype(jnp.float32).reshape(B, S, S5_GROUPS, S5_GROUP)
    dt = jnp.exp(log_dt.astype(jnp.float32))[:, None]
    lr = lam_re.astype(jnp.float32)
    li = lam_im.astype(jnp.float32)
    mag = jnp.exp(lr * dt)
    ar = mag * jnp.cos(li * dt)
    ai = mag * jnp.sin(li * dt)
    den = lr * lr + li * li
    fr = ((ar - 1.0) * lr + ai * li) / den
    fi = (ai * lr - (ar - 1.0) * li) / den
    br_ = b_re.astype(jnp.float32)
    bi_ = b_im.astype(jnp.float32)
    bbr = fr[..., None] * br_ - fi[..., None] * bi_
    bbi = fr[..., None] * bi_ + fi[..., None] * br_
    cr = c_re.astype(jnp.float32)
    ci = c_im.astype(jnp.float32)

    def combine(e1, e2):
        a1r, a1i, b1r, b1i = e1
        a2r, a2i, b2r, b2i = e2
        return (a1r * a2r - a1i * a2i, a1r * a2i + a1i * a2r,
                a2r * b1r - a2i * b1i + b2r, a2r * b1i + a2i * b1r + b2i)

    nc = S // S5_CHUNK
    uc = ug.reshape(B, nc, S5_CHUNK, S5_GROUPS, S5_GROUP).transpose(1, 0, 2, 3, 4)

    def chunk_step(carry, u_chunk):
        hr0, hi0 = carry
        bur = jnp.einsum('btgc,gpc->btgp', u_chunk, bbr)
        bui = jnp.einsum('btgc,gpc->btgp', u_chunk, bbi)
        a_r = jnp.broadcast_to(ar, bur.shape)
        a_i = jnp.broadcast_to(ai, bur.shape)
        pr, pi, sr, si = lax.associative_scan(combine, (a_r, a_i, bur, bui), axis=1)
        hr = sr + pr * hr0[:, None] - pi * hi0[:, None]
        hi = si + pr * hi0[:, None] + pi * hr0[:, None]
        y = jnp.einsum('btgp,gcp->btgc', hr, cr) - jnp.einsum('btgp,gcp->btgc', hi, ci)
        return (hr[:, -1], hi[:, -1]), y

    h0 = jnp.zeros((B, S5_GROUPS, S5_STATE), jnp.float32)
    _, ys = lax.scan(chunk_step, (h0, h0), uc)
    y = ys.transpose(1, 0, 2, 3, 4).reshape(B, S, D_MODEL)
    y = y + d_skip.astype(jnp.float32) * u.astype(jnp.float32)
    z = jax.nn.gelu(y).astype(h.dtype)
    a, g = jnp.split(z @ w_glu, 2, axis=-1)
    return a * jax.nn.sigmoid(g)


def dsa_mixer(h, positions, w_in, q_gain, k_gain, w_out):
    B, S, _ = h.shape
    D = D_MODEL
    cuts = [D, 2 * D, 3 * D, 3 * D + IDX_HEADS * IDX_DIM, 3 * D + IDX_HEADS * IDX_DIM + IDX_DIM]
    q, k, v, qi, ki, wi = jnp.split(h @ w_in, cuts, axis=-1)
    q = rope(rmsnorm(q.reshape(B, S, N_HEADS, HEAD_DIM), q_gain), positions)
    k = rope(rmsnorm(k.reshape(B, S, N_HEADS, HEAD_DIM), k_gain), positions)
    v = v.reshape(B, S, N_HEADS, HEAD_DIM)
    qi = rope(qi.reshape(B, S, IDX_HEADS, IDX_DIM), positions)
    ki = rope(ki.reshape(B, S, 1, IDX_DIM), positions)[:, :, 0]
    wi = wi * IDX_HEADS ** -0.5
    topk = min(DSA_TOPK_MAX, S // 4)
    nb = S // DSA_Q_BLOCK
    qb = q.reshape(B, nb, DSA_Q_BLOCK, N_HEADS, HEAD_DIM).transpose(1, 0, 2, 3, 4)
    qib = qi.reshape(B, nb, DSA_Q_BLOCK, IDX_HEADS, IDX_DIM).transpose(1, 0, 2, 3, 4)
    wib = wi.reshape(B, nb, DSA_Q_BLOCK, IDX_HEADS).transpose(1, 0, 2, 3)
    kpos = jnp.arange(S)
    gather = jax.vmap(lambda arr, idx: arr[idx])

    def block(args):
        qblk, qiblk, wiblk, start = args
        qpos = start + jnp.arange(DSA_Q_BLOCK)
        rel = jnp.einsum('bthd,bsd->bths', qiblk, ki, preferred_element_type=jnp.float32) * IDX_DIM ** -0.5
        score = jnp.einsum('bth,bths->bts', wiblk.astype(jnp.float32), jax.nn.relu(rel))
        causal = kpos[None, :] <= qpos[:, None]
        score = jnp.where(causal[None], score, -jnp.inf)
        _, idx = lax.top_k(score, topk)
        valid = idx <= qpos[None, :, None]
        k_sel = gather(k, idx)
        v_sel = gather(v, idx)
        logits = jnp.einsum('bthd,btkhd->bhtk', qblk, k_sel, preferred_element_type=jnp.float32) * HEAD_DIM ** -0.5
        logits = jnp.where(valid[:, None], logits, -jnp.inf)
        p = jax.nn.softmax(logits, axis=-1)
        return jnp.einsum('bhtk,btkhd->bthd', p.astype(v.dtype), v_sel)

    starts = jnp.arange(nb) * DSA_Q_BLOCK
    o = lax.map(block, (qb, qib, wib, starts))
    o = o.transpose(1, 0, 2, 3, 4).reshape(B, S, D_MODEL)
    return o @ w_out


def setup_inputs(seed: int = 0) -> dict:
    key = jax.random.key(seed)
    ks = iter(jax.random.split(key, 32))
    f32 = jnp.float32

    def nrm(shape, scale):
        return jax.random.normal(next(ks), shape, f32) * scale

    D = D_MODEL
    n_sb = len(range(0, DEPTH, N_MIXERS))
    n_s5 = len(range(1, DEPTH, N_MIXERS))
    n_dsa = len(range(2, DEPTH, N_MIXERS))
    G, P, Gc = S5_GROUPS, S5_STATE, S5_GROUP
    x = nrm((BATCH, SEQ, D), 1.0)
    c = nrm((BATCH, D), 1.0)
    positions = jnp.tile(jnp.arange(SEQ, dtype=jnp.int32)[None, :], (BATCH, 1))
    ln1_g = 1.0 + nrm((DEPTH, D), 0.01)
    ln2_g = 1.0 + nrm((DEPTH, D), 0.01)
    ada_w = nrm((DEPTH, D, 6 * D), 0.5 * D ** -0.5)
    ada_b = nrm((DEPTH, 6 * D), 0.02)
    mlp_w1 = nrm((DEPTH, D, D_FF), D ** -0.5)
    mlp_w2 = nrm((DEPTH, D_FF, D), D_FF ** -0.5)
    sb_w_in = nrm((n_sb, D, 3 * D), D ** -0.5)
    sb_q_gain = 1.0 + nrm((n_sb, HEAD_DIM), 0.01)
    sb_k_gain = 1.0 + nrm((n_sb, HEAD_DIM), 0.01)
    sb_w_out = nrm((n_sb, D, D), D ** -0.5)
    s5_w_in = nrm((n_s5, D, D), D ** -0.5)
    s5_lambda_re = -0.5 + nrm((n_s5, G, P), 0.01)
    s5_lambda_im = jnp.pi * jnp.arange(P, dtype=f32)[None, None, :] + nrm((n_s5, G, P), 0.01)
    s5_log_dt = jax.random.uniform(next(ks), (n_s5, G), f32, math.log(S5_DT_MIN), math.log(S5_DT_MAX))
    s5_b_re = nrm((n_s5, G, P, Gc), Gc ** -0.5)
    s5_b_im = nrm((n_s5, G, P, Gc), Gc ** -0.5)
    s5_c_re = nrm((n_s5, G, Gc, P), P ** -0.5)
    s5_c_im = nrm((n_s5, G, Gc, P), P ** -0.5)
    s5_d = nrm((n_s5, D), 1.0)
    s5_w_glu = nrm((n_s5, D, 2 * D), D ** -0.5)
    dsa_w_in = nrm((n_dsa, D, DSA_IN), D ** -0.5)
    dsa_q_gain = 1.0 + nrm((n_dsa, HEAD_DIM), 0.01)
    dsa_k_gain = 1.0 + nrm((n_dsa, HEAD_DIM), 0.01)
    dsa_w_out = nrm((n_dsa, D, D), D ** -0.5)
    return {"x": x, "c": c, "positions": positions,
            "ln1_g": ln1_g, "ln2_g": ln2_g, "ada_w": ada_w, "ada_b": ada_b,
            "mlp_w1": mlp_w1, "mlp_w2": mlp_w2,
            "sb_w_in": sb_w_in, "sb_q_gain": sb_q_gain, "sb_k_gain": sb_k_gain, "sb_w_out": sb_w_out,
            "s5_w_in": s5_w_in, "s5_lambda_re": s5_lambda_re, "s5_lambda_im": s5_lambda_im,
            "s5_log_dt": s5_log_dt, "s5_b_re": s5_b_re, "s5_b_im": s5_b_im,
            "s5_c_re": s5_c_re, "s5_c_im": s5_c_im, "s5_d": s5_d, "s5_w_glu": s5_w_glu,
            "dsa_w_in": dsa_w_in, "dsa_q_gain": dsa_q_gain, "dsa_k_gain": dsa_k_gain, "dsa_w_out": dsa_w_out}


def reference(x, c, positions, ln1_g, ln2_g, ada_w, ada_b, mlp_w1, mlp_w2,
              sb_w_in, sb_q_gain, sb_k_gain, sb_w_out,
              s5_w_in, s5_lambda_re, s5_lambda_im, s5_log_dt, s5_b_re, s5_b_im,
              s5_c_re, s5_c_im, s5_d, s5_w_glu,
              dsa_w_in, dsa_q_gain, dsa_k_gain, dsa_w_out):
    cond = jax.nn.silu(c)
    counts = [0, 0, 0]
    for i in range(DEPTH):
        mod = cond @ ada_w[i] + ada_b[i]
        sh1, sc1, g1, sh2, sc2, g2 = jnp.split(mod, 6, axis=-1)
        h = modulate(rmsnorm(x, ln1_g[i]), sh1, sc1)
        kind = i % N_MIXERS
        j = counts[kind]
        counts[kind] += 1
        if kind == 0:
            y = stick_breaking_mixer(h, sb_w_in[j], sb_q_gain[j], sb_k_gain[j], sb_w_out[j])
        elif kind == 1:
            y = s5_mixer(h, s5_w_in[j], s5_lambda_re[j], s5_lambda_im[j], s5_log_dt[j],
                         s5_b_re[j], s5_b_im[j], s5_c_re[j], s5_c_im[j], s5_d[j], s5_w_glu[j])
        else:
            y = dsa_mixer(h, positions, dsa_w_in[j], dsa_q_gain[j], dsa_k_gain[j], dsa_w_out[j])
        x = x + g1[:, None, :] * y
        h = modulate(rmsnorm(x, ln2_g[i]), sh2, sc2)
        x = x + g2[:, None, :] * sq_relu_mlp(h, mlp_w1[i], mlp_w2[i])
    return x
```

```python
import contextlib
import numpy as np
import ml_dtypes
import concourse.bass as bass
import concourse.mybir as mybir
from concourse.bass_utils import run_bass_kernel_spmd

F32 = mybir.dt.float32
BF16 = mybir.dt.bfloat16
AF = mybir.ActivationFunctionType
ALU = mybir.AluOpType
AX = mybir.AxisListType
NPBF = ml_dtypes.bfloat16

D = 2048
DC = 16
DFF = 8192
FC = 64
B = 4
S = 4096
NH = 16
HD = 128
EPS = 1e-6
NCORES = 8


class Trk:
    __slots__ = ("w", "r")

    def __init__(self):
        self.w = None
        self.r = {}


class Prog:
    def __init__(self):
        self.nc = bass.Bass("TRN2", target_bir_lowering=False)
        nc = self.nc
        self.es = contextlib.ExitStack()
        self.eng = {"pe": nc.tensor, "act": nc.scalar, "dve": nc.vector, "pool": nc.gpsimd, "sp": nc.sync}
        self.sems = {}
        self.cnt = {}
        self.known = {}
        for e in self.eng:
            self.sems[e] = self.es.enter_context(nc.semaphore("s_" + e))
            self.cnt[e] = 0
            self.known[e] = {}
        self.ndma = 0
        self.uid = 0

    def sb(self, shape, dt, name=None):
        self.uid += 1
        return self.es.enter_context(self.nc.sbuf_tensor(name or f"sb{self.uid}", list(shape), dt))

    def ps(self, shape=(128, 512), dt=F32, name=None):
        self.uid += 1
        return self.es.enter_context(self.nc.psum_tensor(name or f"ps{self.uid}", list(shape), dt))

    def dram(self, name, shape, dt, kind):
        return self.nc.dram_tensor(name, list(shape), dt, kind=kind).ap()

    def dsem(self):
        self.ndma += 1
        k = f"d{self.ndma}"
        self.sems[k] = self.es.enter_context(self.nc.semaphore(k))
        self.cnt[k] = 0
        return k

    def _waits(self, e, reads, writes):
        need = {}
        for t in reads:
            if t.w is not None:
                k, v = t.w
                if need.get(k, 0) < v:
                    need[k] = v
        for t in writes:
            if t.w is not None:
                k, v = t.w
                if need.get(k, 0) < v:
                    need[k] = v
            for k, v in t.r.items():
                if need.get(k, 0) < v:
                    need[k] = v
        kn = self.known[e]
        eng = self.eng[e]
        for k, v in need.items():
            if k == "pe" and e == "pe":
                continue
            if k not in self.eng:
                v = self.cnt[k]
            if kn.get(k, 0) < v:
                eng.wait_ge(self.sems[k], v)
                kn[k] = v

    def op(self, e, fn, reads=(), writes=()):
        self._waits(e, reads, writes)
        ins = fn(self.eng[e])
        self.cnt[e] += 1
        v = self.cnt[e]
        ins.then_inc(self.sems[e], 1)
        for t in reads:
            t.r[e] = v
        for t in writes:
            t.w = (e, v)
            t.r = {}
        return ins

    def dma(self, q, out, in_, sem, reads=(), writes=()):
        self._waits(q, reads, writes)
        ins = self.eng[q].dma_start(out=out, in_=in_)
        self.cnt[sem] += 16
        v = self.cnt[sem]
        ins.then_inc(self.sems[sem], 16)
        for t in reads:
            t.r[sem] = v
        for t in writes:
            t.w = (sem, v)
            t.r = {}
        return ins

    def finish(self, trks):
        need = {}
        for t in trks:
            if t.w is not None:
                need[t.w[0]] = max(need.get(t.w[0], 0), t.w[1])
            for k, v in t.r.items():
                need[k] = max(need.get(k, 0), v)
        for k, v in need.items():
            self.eng["sp"].wait_ge(self.sems[k], v)
        self.es.close()
        return self.nc


class Slots:
    def __init__(self, P, n, shape, dt, name):
        self.tiles = [P.sb(shape, dt, f"{name}{i}") for i in range(n)]
        self.trk = [Trk() for _ in range(n)]
        self.sem = [P.dsem() for _ in range(n)]
        self.i = 0
        self.n = n

    def next(self):
        j = self.i % self.n
        self.i += 1
        return self.tiles[j], self.trk[j], self.sem[j]


class PsumRing:
    def __init__(self, P, n, name="pr"):
        self.tiles = [P.ps(name=f"{name}{i}") for i in range(n)]
        self.trk = [Trk() for _ in range(n)]
        self.i = 0
        self.n = n

    def next(self):
        j = self.i % self.n
        self.i += 1
        return self.tiles[j], self.trk[j]


def lay_w(w):
    K, N = w.shape
    return np.ascontiguousarray(w.reshape(K // 128, 128, N // 128, 128).transpose(2, 1, 0, 3))


def lay_vec(v):
    return np.ascontiguousarray(v.reshape(-1, 128).T)


def build_mlp(T, glu):
    P = Prog()
    TB = 512
    NB = T // TB
    NO = 32 if glu else 16
    xT = P.dram("xT", [D, T], F32, "ExternalInput")
    oT = P.dram("oT", [D, T], BF16, "ExternalInput")
    wo = P.dram("wo", [NO, 128, DC, 128], BF16, "ExternalInput")
    w1 = P.dram("w1", [FC, 128, DC, 128], BF16, "ExternalInput")
    w2 = P.dram("w2", [DC, 128, FC, 128], BF16, "ExternalInput")
    vec = P.dram("vec", [128, 5, DC], F32, "ExternalInput")
    yT = P.dram("yT", [D, T], F32, "ExternalOutput")
    xTv = xT.rearrange("(c p) t -> p c t", p=128)
    oTv = oT.rearrange("(c p) t -> p c t", p=128)
    yTv = yT.rearrange("(c p) t -> p c t", p=128)

    vt = P.sb([128, 5, DC], F32, "vt")
    vt_t = Trk()
    vsem = P.dsem()
    gm = P.sb([128, DC], F32, "gm")
    gm_t = Trk()
    ones = P.sb([128, 128], BF16, "ones")
    ones_t = Trk()
    P.dma("sp", vt[:], vec, vsem, writes=[vt_t])
    P.op("dve", lambda e: e.memset(ones[:], 1.0), writes=[ones_t])
    P.op("dve", lambda e: e.scalar_tensor_tensor(out=gm[:], in0=vt[:, 2, :], scalar=1.0, in1=vt[:, 1, :],
                                                 op0=ALU.add, op1=ALU.mult), reads=[vt_t], writes=[gm_t])

    xs = Slots(P, 2, [128, DC, TB], F32, "xs")
    os_ = Slots(P, 1, [128, DC, TB], BF16, "os")
    ws16 = Slots(P, 3, [128, DC, 128], BF16, "ws")
    ws64 = Slots(P, 2, [128, FC // 2, 128], BF16, "wl")
    hT = P.sb([128, DC, TB], BF16, "hT")
    hT_t = Trk()
    aT = P.sb([128, FC, TB], BF16, "aT")
    aT_t = [Trk() for _ in range(FC)]
    sq = hT
    sq_t = hT_t
    tmp = [P.sb([128, TB], F32, f"tmp{i}") for i in range(3)]
    tmp_t = [Trk() for _ in range(3)]
    rstd = P.sb([128, TB], F32, "rstd")
    rstd_t = Trk()
    pr = PsumRing(P, 6)
    pstat = P.ps(name="pstat")
    pstat_t = Trk()
    ti = [0]

    def nexttmp():
        j = ti[0] % 3
        ti[0] += 1
        return tmp[j], tmp_t[j]

    for blk in range(NB):
        t0 = blk * TB
        xb, xb_t, xsem = xs.next()
        ob, ob_t, osem = os_.next()
        P.dma("sp", xb[:], xTv[:, :, t0:t0 + TB], xsem, writes=[xb_t])
        P.dma("sp", ob[:], oTv[:, :, t0:t0 + TB], osem, writes=[ob_t])
        if not glu:
            for n in range(DC):
                wt, wt_t, wsem = ws16.next()
                P.dma("sp", wt[:], wo[n], wsem, writes=[wt_t])
                pt, pt_t = pr.next()
                for k in range(DC):
                    P.op("pe", lambda e, k=k: e.matmul(pt[:], wt[:, k, :], ob[:, k, :], start=(k == 0), stop=(k == DC - 1)),
                         reads=[wt_t, ob_t], writes=[pt_t])
                P.op("dve", lambda e: e.scalar_tensor_tensor(out=xb[:, n, :], in0=pt[:], scalar=vt[:, 0, n:n + 1], in1=xb[:, n, :],
                                                             op0=ALU.mult, op1=ALU.add),
                     reads=[pt_t, vt_t], writes=[xb_t])
        else:
            for n in range(DC):
                pts = []
                for half in range(2):
                    wt, wt_t, wsem = ws16.next()
                    P.dma("sp", wt[:], wo[n + DC * half], wsem, writes=[wt_t])
                    pt, pt_t = pr.next()
                    for k in range(DC):
                        P.op("pe", lambda e, k=k: e.matmul(pt[:], wt[:, k, :], ob[:, k, :], start=(k == 0), stop=(k == DC - 1)),
                             reads=[wt_t, ob_t], writes=[pt_t])
                    pts.append((pt, pt_t))
                (pa, pa_t), (pg, pg_t) = pts
                tg, tg_t = nexttmp()
                P.op("act", lambda e: e.activation(out=tg[:], in_=pg[:], func=AF.Sigmoid), reads=[pg_t], writes=[tg_t])
                P.op("dve", lambda e: e.tensor_tensor(out=tg[:], in0=pa[:], in1=tg[:], op=ALU.mult), reads=[pa_t, tg_t], writes=[tg_t])
                P.op("dve", lambda e: e.scalar_tensor_tensor(out=xb[:, n, :], in0=tg[:], scalar=vt[:, 0, n:n + 1], in1=xb[:, n, :],
                                                             op0=ALU.mult, op1=ALU.add),
                     reads=[tg_t, vt_t], writes=[xb_t])
        P.op("act", lambda e: e.activation(out=sq[:], in_=xb[:], func=AF.Square), reads=[xb_t], writes=[sq_t])
        for k in range(DC):
            P.op("pe", lambda e, k=k: e.matmul(pstat[:], ones[:], sq[:, k, :], start=(k == 0), stop=(k == DC - 1)),
                 reads=[ones_t, sq_t], writes=[pstat_t])
        P.op("dve", lambda e: e.tensor_scalar(out=rstd[:], in0=pstat[:], scalar1=1.0 / D, scalar2=EPS, op0=ALU.mult, op1=ALU.add),
             reads=[pstat_t], writes=[rstd_t])
        P.op("act", lambda e: e.activation(out=rstd[:], in_=rstd[:], func=AF.Sqrt), reads=[rstd_t], writes=[rstd_t])
        P.op("dve", lambda e: e.reciprocal(out=rstd[:], in_=rstd[:]), reads=[rstd_t], writes=[rstd_t])
        for c in range(DC):
            tt, tt_t = nexttmp()
            P.op("dve", lambda e: e.tensor_tensor(out=tt[:], in0=xb[:, c, :], in1=rstd[:], op=ALU.mult),
                 reads=[xb_t, rstd_t], writes=[tt_t])
            P.op("act", lambda e: e.activation(out=hT[:, c, :], in_=tt[:], func=AF.Identity, scale=gm[:, c:c + 1], bias=vt[:, 3, c:c + 1]),
                 reads=[tt_t, gm_t, vt_t], writes=[hT_t])
        for n in range(FC):
            wt, wt_t, wsem = ws16.next()
            P.dma("sp", wt[:], w1[n], wsem, writes=[wt_t])
            pt, pt_t = pr.next()
            for k in range(DC):
                P.op("pe", lambda e, k=k: e.matmul(pt[:], wt[:, k, :], hT[:, k, :], start=(k == 0), stop=(k == DC - 1)),
                     reads=[wt_t, hT_t], writes=[pt_t])
            tt, tt_t = nexttmp()
            P.op("act", lambda e: e.activation(out=tt[:], in_=pt[:], func=AF.Relu), reads=[pt_t], writes=[tt_t])
            P.op("dve", lambda e: e.tensor_tensor(out=aT[:, n, :], in0=tt[:], in1=tt[:], op=ALU.mult), reads=[tt_t], writes=[aT_t[n]])
        for n in range(DC):
            pt, pt_t = pr.next()
            for hf in range(2):
                wt, wt_t, wsem = ws64.next()
                P.dma("sp", wt[:], w2[n, :, hf * 32:(hf + 1) * 32, :], wsem, writes=[wt_t])
                for kk in range(FC // 2):
                    k = hf * 32 + kk
                    P.op("pe", lambda e: e.matmul(pt[:], wt[:, kk, :], aT[:, k, :], start=(k == 0), stop=(k == FC - 1)),
                         reads=[wt_t, aT_t[k]], writes=[pt_t])
            P.op("dve", lambda e: e.scalar_tensor_tensor(out=xb[:, n, :], in0=pt[:], scalar=vt[:, 4, n:n + 1], in1=xb[:, n, :],
                                                         op0=ALU.mult, op1=ALU.add),
                 reads=[pt_t, vt_t], writes=[xb_t])
        P.dma("sp", yTv[:, :, t0:t0 + TB], xb[:], xsem, reads=[xb_t])
    return P.finish(xs.trk)


I32 = mybir.dt.int32
TWO_PI = float(2 * np.pi)


class Ring:
    def __init__(self, P, n, shape, dt, name):
        self.tiles = [P.sb(shape, dt, f"{name}{i}") for i in range(n)]
        self.trk = [Trk() for _ in range(n)]
        self.i = 0
        self.n = n

    def next(self):
        j = self.i % self.n
        self.i += 1
        return self.tiles[j], self.trk[j]


def emit_sin(P, out, out_t, ang, ang_t, shape, ring_f, ring_i, shift=0.0, eng="dve"):
    kf, kf_t = ring_f.next()
    ki, ki_t = ring_i.next()
    sl = tuple(slice(0, s) for s in shape[1:])
    kfv = kf[(slice(0, shape[0]),) + sl]
    kiv = ki[(slice(0, shape[0]),) + sl]
    P.op(eng, lambda e: e.tensor_scalar(out=kfv, in0=ang, scalar1=shift, scalar2=1.0 / TWO_PI, op0=ALU.add, op1=ALU.mult),
         reads=[ang_t], writes=[kf_t])
    P.op(eng, lambda e: e.tensor_copy(out=kiv, in_=kfv), reads=[kf_t], writes=[ki_t])
    P.op(eng, lambda e: e.tensor_copy(out=kfv, in_=kiv), reads=[ki_t], writes=[kf_t])
    P.op(eng, lambda e: e.scalar_tensor_tensor(out=kfv, in0=kfv, scalar=-TWO_PI, in1=ang, op0=ALU.mult, op1=ALU.add),
         reads=[kf_t, ang_t], writes=[kf_t])
    if shift != 0.0:
        P.op(eng, lambda e: e.tensor_scalar(out=kfv, in0=kfv, scalar1=shift, scalar2=None, op0=ALU.add), reads=[kf_t], writes=[kf_t])
    P.op(eng, lambda e: e.tensor_scalar(out=kfv, in0=kfv, scalar1=float(np.pi), scalar2=-float(np.pi), op0=ALU.min, op1=ALU.max),
         reads=[kf_t], writes=[kf_t])
    P.op("act", lambda e: e.activation(out=out, in_=kfv, func=AF.Sin), reads=[kf_t], writes=[out_t])


def build_proj(T, items, NWF, NOUT, out_dt, NT, NG, use_rope):
    P = Prog()
    TB = 512
    NB = T // TB
    xT = P.dram("xT", [D, T], F32, "ExternalInput")
    vec = P.dram("vec", [128, 3, DC], F32, "ExternalInput")
    wf = P.dram("wf", [NWF, 128, DC, 128], BF16, "ExternalInput")
    yT = P.dram("yT", [NOUT * 128, T], out_dt, "ExternalOutput")
    if NT:
        wt_d = P.dram("wt", [NT, 128, DC, 512], BF16, "ExternalInput")
        vo = P.dram("vo", [T, NT * 512], BF16, "ExternalOutput")
    gv_d = P.dram("gv", [128, max(NG, 1)], F32, "ExternalInput")
    if use_rope:
        pos = P.dram("pos", [1, T], I32, "ExternalInput")
        fv_d = P.dram("fv", [128, 4], F32, "ExternalInput")
    xTv = xT.rearrange("(c p) t -> p c t", p=128)

    vt = P.sb([128, 3, DC], F32, "vt"); vt_t = Trk()
    gv = P.sb([128, max(NG, 1)], F32, "gvs"); gv_t = Trk()
    gm = P.sb([128, DC], F32, "gm"); gm_t = Trk()
    ones = P.sb([128, 128], BF16, "ones"); ones_t = Trk()
    s0 = P.dsem()
    P.dma("sp", vt[:], vec, s0, writes=[vt_t])
    P.dma("sp", gv[:], gv_d, s0, writes=[gv_t])
    P.op("dve", lambda e: e.memset(ones[:], 1.0), writes=[ones_t])
    P.op("dve", lambda e: e.scalar_tensor_tensor(out=gm[:], in0=vt[:, 1, :], scalar=1.0, in1=vt[:, 0, :],
                                                 op0=ALU.add, op1=ALU.mult), reads=[vt_t], writes=[gm_t])
    tmpr = Ring(P, 4, [128, TB], F32, "tmp")
    tabs = {}
    if use_rope:
        fv = P.sb([128, 4], F32, "fvs"); fv_t = Trk()
        P.dma("sp", fv[:], fv_d, s0, writes=[fv_t])
        posi = P.sb([128, T], I32, "posi"); posi_t = Trk()
        P.dma("sp", posi[:], pos.broadcast_to([128, T]), s0, writes=[posi_t])
        posf = P.sb([128, T], F32, "posf"); posf_t = Trk()
        P.op("dve", lambda e: e.tensor_copy(out=posf[:], in_=posi[:]), reads=[posi_t], writes=[posf_t])
        ring_i = Ring(P, 2, [128, TB], I32, "ri")
        ang = P.sb([128, TB], F32, "ang"); ang_t = Trk()
        for nm, fcol, scol in (("m", 0, 1), ("i", 2, 3)):
            ct = P.sb([128, T], F32, "cos" + nm); ct_t = Trk()
            st = P.sb([128, T], F32, "sin" + nm); st_t = Trk()
            for b0 in range(0, T, TB):
                P.op("dve", lambda e: e.tensor_scalar(out=ang[:], in0=posf[:, b0:b0 + TB], scalar1=fv[:, fcol:fcol + 1], scalar2=None, op0=ALU.mult),
                     reads=[posf_t, fv_t], writes=[ang_t])
                emit_sin(P, ct[:, b0:b0 + TB], ct_t, ang[:], ang_t, [128, TB], tmpr, ring_i, shift=float(np.pi / 2))
                emit_sin(P, st[:, b0:b0 + TB], st_t, ang[:], ang_t, [128, TB], tmpr, ring_i)
                P.op("dve", lambda e: e.tensor_scalar(out=st[:, b0:b0 + TB], in0=st[:, b0:b0 + TB], scalar1=fv[:, scol:scol + 1], scalar2=None, op0=ALU.mult),
                     reads=[st_t, fv_t], writes=[st_t])
            tabs[nm] = (ct, ct_t, st, st_t)

    xs = Slots(P, 2, [128, DC, TB], F32, "xs")
    ws = Slots(P, 4, [128, DC, 128], BF16, "ws")
    if NT:
        wts = Slots(P, 2, [128, DC, 512], BF16, "wts")
        vst = Slots(P, 3, [128, 512], BF16, "vst")
    hT = P.sb([128, DC, TB], BF16, "hT"); hT_t = Trk()
    rstd = P.sb([128, TB], F32, "rstd"); rstd_t = Trk()
    rs2 = Ring(P, 2, [128, TB], F32, "rs2")
    sqr = Ring(P, 2, [128, TB], BF16, "sqr")
    ost = Slots(P, 3, [128, TB], out_dt, "ost")
    pr = PsumRing(P, 5)
    pstat = PsumRing(P, 2, "pst")

    for blk in range(NB):
        t0 = blk * TB
        xb, xb_t, xsem = xs.next()
        P.dma("sp", xb[:], xTv[:, :, t0:t0 + TB], xsem, writes=[xb_t])
        P.op("act", lambda e: e.activation(out=hT[:], in_=xb[:], func=AF.Square), reads=[xb_t], writes=[hT_t])
        pst, pst_t = pstat.next()
        for k in range(DC):
            P.op("pe", lambda e: e.matmul(pst[:], ones[:], hT[:, k, :], start=(k == 0), stop=(k == DC - 1)),
                 reads=[ones_t, hT_t], writes=[pst_t])
        P.op("dve", lambda e: e.tensor_scalar(out=rstd[:], in0=pst[:], scalar1=1.0 / D, scalar2=EPS, op0=ALU.mult, op1=ALU.add),
             reads=[pst_t], writes=[rstd_t])
        P.op("act", lambda e: e.activation(out=rstd[:], in_=rstd[:], func=AF.Sqrt), reads=[rstd_t], writes=[rstd_t])
        P.op("dve", lambda e: e.reciprocal(out=rstd[:], in_=rstd[:]), reads=[rstd_t], writes=[rstd_t])
        for c in range(DC):
            tt, tt_t = tmpr.next()
            P.op("dve", lambda e: e.tensor_tensor(out=tt[:], in0=xb[:, c, :], in1=rstd[:], op=ALU.mult),
                 reads=[xb_t, rstd_t], writes=[tt_t])
            P.op("act", lambda e: e.activation(out=hT[:, c, :], in_=tt[:], func=AF.Identity, scale=gm[:, c:c + 1], bias=vt[:, 2, c:c + 1]),
                 reads=[tt_t, gm_t, vt_t], writes=[hT_t])

        def mm_chunk(widx):
            wt, wt_t, wsem = ws.next()
            P.dma("sp", wt[:], wf[widx], wsem, writes=[wt_t])
            pt, pt_t = pr.next()
            for k in range(DC):
                P.op("pe", lambda e: e.matmul(pt[:], wt[:, k, :], hT[:, k, :], start=(k == 0), stop=(k == DC - 1)),
                     reads=[wt_t, hT_t], writes=[pt_t])
            return pt, pt_t

        for it in items:
            mode = it["mode"]
            pt, pt_t = mm_chunk(it["w"])
            ot, ot_t, osem = ost.next()
            if mode == "plain":
                P.op("act", lambda e: e.activation(out=ot[:], in_=pt[:], func=AF.Copy, scale=float(it.get("scale", 1.0))),
                     reads=[pt_t], writes=[ot_t])
            else:
                if mode in ("hn", "hn_rope"):
                    sq, sq_t = sqr.next()
                    P.op("act", lambda e: e.activation(out=sq[:], in_=pt[:], func=AF.Square), reads=[pt_t], writes=[sq_t])
                    p2, p2_t = pstat.next()
                    P.op("pe", lambda e: e.matmul(p2[:], ones[:], sq[:], start=True, stop=True), reads=[ones_t, sq_t], writes=[p2_t])
                    r2, r2_t = rs2.next()
                    P.op("dve", lambda e: e.tensor_scalar(out=r2[:], in0=p2[:], scalar1=1.0 / HD, scalar2=EPS, op0=ALU.mult, op1=ALU.add),
                         reads=[p2_t], writes=[r2_t])
                    P.op("act", lambda e: e.activation(out=r2[:], in_=r2[:], func=AF.Sqrt), reads=[r2_t], writes=[r2_t])
                    P.op("dve", lambda e: e.reciprocal(out=r2[:], in_=r2[:]), reads=[r2_t], writes=[r2_t])
                if mode == "hn":
                    g = it["g"]
                    P.op("dve", lambda e: e.scalar_tensor_tensor(out=ot[:], in0=pt[:], scalar=gv[:, g:g + 1], in1=r2[:], op0=ALU.mult, op1=ALU.mult),
                         reads=[pt_t, gv_t, r2_t], writes=[ot_t])
                else:
                    pts, pts_t = mm_chunk(it["ws"])
                    ct, ct_t, st, st_t = tabs["m" if mode == "hn_rope" else "i"]
                    u1, u1_t = tmpr.next()
                    u2, u2_t = tmpr.next()
                    if mode == "hn_rope":
                        g, gs = it["g"], it["gs"]
                        P.op("dve", lambda e: e.scalar_tensor_tensor(out=u1[:], in0=pt[:], scalar=gv[:, g:g + 1], in1=ct[:, t0:t0 + TB], op0=ALU.mult, op1=ALU.mult),
                             reads=[pt_t, gv_t, ct_t], writes=[u1_t])
                        P.op("dve", lambda e: e.scalar_tensor_tensor(out=u2[:], in0=pts[:], scalar=gv[:, gs:gs + 1], in1=st[:, t0:t0 + TB], op0=ALU.mult, op1=ALU.mult),
                             reads=[pts_t, gv_t, st_t], writes=[u2_t])
                        P.op("pool", lambda e: e.tensor_tensor(out=u1[:], in0=u1[:], in1=u2[:], op=ALU.add), reads=[u1_t, u2_t], writes=[u1_t])
                        P.op("dve", lambda e: e.tensor_tensor(out=ot[:], in0=u1[:], in1=r2[:], op=ALU.mult), reads=[u1_t, r2_t], writes=[ot_t])
                    else:
                        P.op("dve", lambda e: e.tensor_tensor(out=u1[:], in0=pt[:], in1=ct[:, t0:t0 + TB], op=ALU.mult), reads=[pt_t, ct_t], writes=[u1_t])
                        P.op("dve", lambda e: e.tensor_tensor(out=u2[:], in0=pts[:], in1=st[:, t0:t0 + TB], op=ALU.mult), reads=[pts_t, st_t], writes=[u2_t])
                        P.op("pool", lambda e: e.tensor_tensor(out=ot[:], in0=u1[:], in1=u2[:], op=ALU.add), reads=[u1_t, u2_t], writes=[ot_t])
            o = it["out"]
            P.dma("sp", yT[o * 128:(o + 1) * 128, t0:t0 + TB], ot[:], osem, reads=[ot_t])
        for g in range(NT):
            wt, wt_t, wsem = wts.next()
            P.dma("sp", wt[:], wt_d[g], wsem, writes=[wt_t])
            for tsb in range(TB // 128):
                pt, pt_t = pr.next()
                for k in range(DC):
                    P.op("pe", lambda e: e.matmul(pt[:], hT[:, k, tsb * 128:(tsb + 1) * 128], wt[:, k, :], start=(k == 0), stop=(k == DC - 1)),
                         reads=[wt_t, hT_t], writes=[pt_t])
                vt_, vt_t_, vsem = vst.next()
                P.op("act", lambda e: e.activation(out=vt_[:], in_=pt[:], func=AF.Copy), reads=[pt_t], writes=[vt_t_])
                P.dma("sp", vo[t0 + tsb * 128:t0 + (tsb + 1) * 128, g * 512:(g + 1) * 512], vt_[:], vsem, reads=[vt_t_])
    fin = list(ost.trk) + (list(vst.trk) if NT else [])
    return P.finish(fin)


def lay_wt(w):
    K, N = w.shape
    return np.ascontiguousarray(w.reshape(K // 128, 128, N // 512, 512).transpose(2, 1, 0, 3))


def _swap_cols(w, width):
    K, N = w.shape
    h = width // 2
    w4 = w.reshape(K, N // width, 2, h)
    return np.ascontiguousarray(w4[:, :, ::-1, :]).reshape(K, N)


def _pad_cols(w, n):
    K, N = w.shape
    if N == n:
        return w
    out = np.zeros((K, n), w.dtype)
    out[:, :N] = w
    return out


def rope_consts():
    p = np.arange(128)
    invf_m = (10000.0 ** (-(2.0 * (p % 64)) / 128.0)).astype(np.float32)
    sgn_m = np.where(p < 64, -1.0, 1.0).astype(np.float32)
    invf_i = (10000.0 ** (-(2.0 * (p % 32)) / 64.0)).astype(np.float32)
    sgn_i = np.where((p % 64) < 32, -1.0, 1.0).astype(np.float32)
    return np.stack([invf_m, sgn_m, invf_i, sgn_i], axis=1)


def proj_plan_sb(w_in, qg, kg):
    chunks, items = [], []
    for h in range(2 * NH):
        chunks.append(w_in[:, h * 128:(h + 1) * 128])
        items.append(dict(mode="hn", w=len(chunks) - 1, g=0 if h < NH else 1, out=h))
    wf = np.stack([lay_w(c)[0] for c in chunks])
    wt = lay_wt(w_in[:, 2 * D:3 * D])
    gv = np.stack([qg, kg], axis=1).astype(np.float32)
    return wf, items, 2 * NH, wt, gv


def proj_plan_s5(w_in):
    chunks, items = [], []
    for c in range(DC):
        chunks.append(w_in[:, c * 128:(c + 1) * 128])
        items.append(dict(mode="plain", w=c, out=c))
    wf = np.stack([lay_w(c)[0] for c in chunks])
    return wf, items, DC, None, np.zeros((128, 1), np.float32)


def proj_plan_dsa(w_in, qg, kg):
    chunks, items = [], []

    def add(c):
        chunks.append(np.ascontiguousarray(c))
        return len(chunks) - 1

    for h in range(2 * NH):
        c = w_in[:, h * 128:(h + 1) * 128]
        isq = h < NH
        items.append(dict(mode="hn_rope", w=add(c), ws=add(_swap_cols(c, 128)), g=0 if isq else 2, gs=1 if isq else 3, out=h))
    o = 2 * NH
    qi0 = 3 * D
    for c8 in range(8):
        c = w_in[:, qi0 + c8 * 128: qi0 + (c8 + 1) * 128]
        items.append(dict(mode="rope_i", w=add(c), ws=add(_swap_cols(c, 64)), out=o + c8))
    ki = w_in[:, qi0 + 1024: qi0 + 1088]
    ki2 = np.concatenate([ki, ki], axis=1)
    items.append(dict(mode="rope_i", w=add(ki2), ws=add(_swap_cols(ki2, 64)), out=o + 8))
    wi = _pad_cols(w_in[:, qi0 + 1088: qi0 + 1104], 128)
    items.append(dict(mode="plain", w=add(wi), out=o + 9, scale=16 ** -0.5))
    wf = np.stack([lay_w(c)[0] for c in chunks])
    wt = lay_wt(w_in[:, 2 * D:3 * D])
    sw = lambda g: np.concatenate([g[64:], g[:64]])
    gv = np.stack([qg, sw(qg), kg, sw(kg)], axis=1).astype(np.float32)
    return wf, items, o + 10, wt, gv


def build_sb(Sq, NHC):
    P = Prog()
    KB = Sq // 128
    QB = Sq // 512
    scale = HD ** -0.5
    qT = P.dram("qT", [NHC, 128, Sq], BF16, "ExternalInput")
    kT = P.dram("kT", [NHC, 128, Sq], BF16, "ExternalInput")
    vv = P.dram("v", [NHC, 128, KB, 128], BF16, "ExternalInput")
    oT = P.dram("oT", [NHC * 128, Sq], BF16, "ExternalOutput")

    ones = P.sb([128, 128], BF16, "ones"); ones_t = Trk()
    tri = P.sb([128, 128], BF16, "tri"); tri_t = Trk()
    P.op("dve", lambda e: e.memset(ones[:], 1.0), writes=[ones_t])
    P.op("pool", lambda e: e.affine_select(out=tri[:], in_=ones[:], pattern=[[-1, 128]], compare_op=ALU.is_gt, fill=0.0, base=0, channel_multiplier=1),
         reads=[ones_t], writes=[tri_t])
    onesW = P.sb([128, 512], BF16, "onesW"); onesW_t = Trk()
    P.op("dve", lambda e: e.memset(onesW[:], 1.0), writes=[onesW_t])
    dmask = []
    dmask_t = Trk()
    for d_ in range(4):
        m_ = P.sb([128, 512], BF16, f"dmask{d_}")
        P.op("pool", lambda e: e.affine_select(out=m_[:], in_=onesW[:], pattern=[[1, 512]], compare_op=ALU.is_gt, fill=0.0,
                                               base=-128 * d_, channel_multiplier=-1), reads=[onesW_t], writes=[dmask_t])
        dmask.append(m_)
    qs = Slots(P, 2, [128, Sq], BF16, "qs")
    ks = Slots(P, 2, [128, Sq], BF16, "ks")
    vs = Slots(P, 2, [128, KB, 128], BF16, "vs")
    er = Ring(P, 2, [128, 512], F32, "e")
    spr = Ring(P, 3, [128, 512], F32, "sp")
    spmr = Ring(P, 3, [128, 512], BF16, "spm")
    cumr = Ring(P, 2, [128, 512], F32, "cum")
    a1r = Ring(P, 3, [128, 512], F32, "a1")
    wr = Ring(P, 3, [128, 512], BF16, "w")
    R = P.sb([128, 512], F32, "R"); R_t = Trk()
    ost = Slots(P, 2, [128, 512], BF16, "ost")
    pz = PsumRing(P, 2, "pz")
    ptri = PsumRing(P, 2, "ptri")
    pone = PsumRing(P, 2, "pone")
    pacc = PsumRing(P, 2, "pacc")

    for h in range(NHC):
        qt, qt_t, qsem = qs.next()
        kt, kt_t, ksem = ks.next()
        vt, vt_t, vsem = vs.next()
        P.dma("sp", qt[:], qT[h], qsem, writes=[qt_t])
        P.dma("sp", kt[:], kT[h], ksem, writes=[kt_t])
        P.dma("sp", vt[:], vv[h], vsem, writes=[vt_t])
        for qb in range(QB):
            q0 = qb * 512
            kmax = 4 * qb + 3
            P.op("pool", lambda e: e.memset(R[:], 0.0), writes=[R_t])
            acc, acc_t = pacc.next()
            for kb in range(kmax, -1, -1):
                diag = kb >= 4 * qb
                z, z_t = pz.next()
                P.op("pe", lambda e: e.matmul(z[:], kt[:, kb * 128:(kb + 1) * 128], qt[:, q0:q0 + 512], start=True, stop=True),
                     reads=[kt_t, qt_t], writes=[z_t])
                ee, ee_t = er.next()
                P.op("act", lambda e: e.activation(out=ee[:], in_=z[:], func=AF.Exp, scale=scale), reads=[z_t], writes=[ee_t])
                sp, sp_t = spr.next()
                P.op("act", lambda e: e.activation(out=sp[:], in_=ee[:], func=AF.Ln, bias=1.0, scale=1.0), reads=[ee_t], writes=[sp_t])
                spm, spm_t = spmr.next()
                base = q0 - kb * 128
                if diag:
                    dm = dmask[kb - 4 * qb]
                    P.op("pool", lambda e: e.tensor_tensor(out=spm[:], in0=sp[:], in1=dm[:], op=ALU.mult), reads=[sp_t, dmask_t], writes=[spm_t])
                else:
                    P.op("pool", lambda e: e.tensor_copy(out=spm[:], in_=sp[:]), reads=[sp_t], writes=[spm_t])
                pt, pt_t = ptri.next()
                P.op("pe", lambda e: e.matmul(pt[:], tri[:], spm[:], start=True, stop=True), reads=[tri_t, spm_t], writes=[pt_t])
                po, po_t = pone.next()
                P.op("pe", lambda e: e.matmul(po[:], ones[:], spm[:], start=True, stop=True), reads=[ones_t, spm_t], writes=[po_t])
                cum, cum_t = cumr.next()
                P.op("dve", lambda e: e.tensor_tensor(out=cum[:], in0=pt[:], in1=R[:], op=ALU.add), reads=[pt_t, R_t], writes=[cum_t])
                a1, a1_t = a1r.next()
                P.op("dve", lambda e: e.scalar_tensor_tensor(out=a1[:], in0=z[:], scalar=scale, in1=sp[:], op0=ALU.mult, op1=ALU.subtract),
                     reads=[z_t, sp_t], writes=[a1_t])
                P.op("pool", lambda e: e.tensor_tensor(out=a1[:], in0=a1[:], in1=cum[:], op=ALU.subtract), reads=[a1_t, cum_t], writes=[a1_t])
                w, w_t = wr.next()
                P.op("act", lambda e: e.activation(out=w[:], in_=a1[:], func=AF.Exp), reads=[a1_t], writes=[w_t])
                if diag:
                    P.op("pool", lambda e: e.tensor_tensor(out=w[:], in0=w[:], in1=dm[:], op=ALU.mult), reads=[w_t, dmask_t], writes=[w_t])
                P.op("pe", lambda e: e.matmul(acc[:], vt[:, kb, :], w[:], start=(kb == kmax), stop=(kb == 0)),
                     reads=[vt_t, w_t], writes=[acc_t])
                if kb > 0:
                    P.op("dve", lambda e: e.tensor_tensor(out=R[:], in0=po[:], in1=R[:], op=ALU.add), reads=[po_t, R_t], writes=[R_t])
            ot, ot_t, osem = ost.next()
            P.op("act", lambda e: e.activation(out=ot[:], in_=acc[:], func=AF.Copy), reads=[acc_t], writes=[ot_t])
            P.dma("sp", oT[h * 128:(h + 1) * 128, q0:q0 + 512], ot[:], osem, reads=[ot_t])
    return P.finish(ost.trk)


def build_dsa(Sq, NHC, TOPK, NIT=24):
    P = Prog()
    KB = Sq // 128
    QG = Sq // 512
    scale = HD ** -0.5
    qT = P.dram("qT", [NHC, 128, Sq], BF16, "ExternalInput")
    kT = P.dram("kT", [NHC, 128, Sq], BF16, "ExternalInput")
    vv = P.dram("v", [NHC, 128, KB, 128], BF16, "ExternalInput")
    qiT = P.dram("qiT", [8, 128, Sq], BF16, "ExternalInput")
    kiT = P.dram("kiT", [128, Sq], BF16, "ExternalInput")
    wi_d = P.dram("wi", [128, KB, 16], BF16, "ExternalInput")
    idn_d = P.dram("idn", [128, 128], BF16, "ExternalInput")
    oT = P.dram("oT", [NHC * 128, Sq], BF16, "ExternalOutput")

    s0 = P.dsem()
    ones = P.sb([128, 128], BF16, "ones"); ones_t = Trk()
    P.op("dve", lambda e: e.memset(ones[:], 1.0), writes=[ones_t])
    idn = P.sb([128, 128], BF16, "idns"); idn_t = Trk()
    P.dma("sp", idn[:], idn_d, s0, writes=[idn_t])
    qi = P.sb([128, 8, Sq], BF16, "qis"); qi_t = Trk()
    P.dma("sp", qi[:], qiT.rearrange("c p s -> p c s"), s0, writes=[qi_t])
    ki = P.sb([128, Sq], BF16, "kis"); ki_t = Trk()
    P.dma("sp", ki[:], kiT, s0, writes=[ki_t])
    wib = P.sb([128, KB, 16], BF16, "wib"); wib_t = Trk()
    P.dma("sp", wib[:], wi_d, s0, writes=[wib_t])
    wi = P.sb([128, KB, 16], F32, "wif"); wi_t = Trk()
    P.op("dve", lambda e: e.tensor_copy(out=wi[:], in_=wib[:]), reads=[wib_t], writes=[wi_t])

    cmask = P.sb([128, 128], F32, "cmask"); cmask_t = Trk()
    P.op("dve", lambda e: e.memset(cmask[:], 3e38), writes=[cmask_t])
    P.op("pool", lambda e: e.affine_select(out=cmask[:], in_=cmask[:], pattern=[[-1, 128]], compare_op=ALU.is_ge, fill=-3e38, base=0, channel_multiplier=1),
         reads=[cmask_t], writes=[cmask_t])
    accr = Ring(P, 2, [128, Sq], F32, "acc")
    mq = P.sb([128, Sq], BF16, "mq"); mq_t = Trk()
    maskT = P.sb([128, KB, 512], BF16, "maskT"); maskT_t = Trk()
    relr = Ring(P, 3, [128, 512], F32, "rel")
    sm = {n: (P.sb([128, 1], F32, "sm_" + n), Trk()) for n in ("mx", "mn", "w0", "lo", "hk", "mid", "cnt", "tmp")}
    ks = Slots(P, 2, [128, Sq], BF16, "ks")
    vs = Slots(P, 2, [128, KB, 128], BF16, "vs")
    qs = Slots(P, 2, [128, 512], BF16, "qs")
    pr_ = Ring(P, 3, [128, 512], BF16, "p")
    pmr = Ring(P, 3, [128, 512], BF16, "pm")
    rd = P.sb([128, 512], F32, "rd"); rd_t = Trk()
    ost = Slots(P, 2, [128, 512], BF16, "ost")
    prel = PsumRing(P, 2, "prel")
    ptr = [P.ps([128, 512], BF16, name=f"ptr{i}") for i in range(2)]
    ptr_t = [Trk(), Trk()]
    pz = PsumRing(P, 2, "pz")
    pacc = P.ps(name="pacc"); pacc_t = Trk()
    pden = P.ps(name="pden"); pden_t = Trk()
    tri_i = [0]

    for qg in range(QG):
        q0g = qg * 512
        kmax = 4 * qg + 3
        for qsub in range(4):
            qt = 4 * qg + qsub
            nk = (qt + 1) * 128
            acc, acc_t = accr.next()
            for g in range((nk + 511) // 512):
                kw = min(512, nk - g * 512)
                for ih in range(16):
                    c, half = ih // 2, ih % 2
                    rows = slice(64 * half, 64 * half + 64)
                    pp, pp_t = prel.next()
                    P.op("pe", lambda e: e.matmul(pp[:, :kw], qi[rows, c, qt * 128:(qt + 1) * 128], ki[rows, g * 512:g * 512 + kw], start=True, stop=True),
                         reads=[qi_t, ki_t], writes=[pp_t])
                    rl, rl_t = relr.next()
                    P.op("act", lambda e: e.activation(out=rl[:, :kw], in_=pp[:, :kw], func=AF.Relu, scale=0.125), reads=[pp_t], writes=[rl_t])
                    dst = acc[:, g * 512:g * 512 + kw]
                    if ih == 0:
                        P.op("dve", lambda e: e.tensor_scalar(out=dst, in0=rl[:, :kw], scalar1=wi[:, qt, 0:1], scalar2=None, op0=ALU.mult),
                             reads=[rl_t, wi_t], writes=[acc_t])
                    else:
                        P.op("dve", lambda e: e.scalar_tensor_tensor(out=dst, in0=rl[:, :kw], scalar=wi[:, qt, ih:ih + 1], in1=dst, op0=ALU.mult, op1=ALU.add),
                             reads=[rl_t, wi_t], writes=[acc_t])
            (mx, mx_t), (mn, mn_t), (w0, w0_t), (lo, lo_t) = sm["mx"], sm["mn"], sm["w0"], sm["lo"]
            (hk, hk_t), (mid, mid_t), (cnt, cnt_t), (tmp, tmp_t) = sm["hk"], sm["mid"], sm["cnt"], sm["tmp"]
            if nk > TOPK:
                P.op("dve", lambda e: e.tensor_reduce(out=mx[:], in_=acc[:, :nk], axis=AX.X, op=ALU.max), reads=[acc_t], writes=[mx_t])
                P.op("dve", lambda e: e.tensor_reduce(out=lo[:], in_=acc[:, :nk], axis=AX.X, op=ALU.min), reads=[acc_t], writes=[lo_t])
                P.op("dve", lambda e: e.scalar_tensor_tensor(out=w0[:], in0=mx[:], scalar=1.0, in1=lo[:], op0=ALU.add, op1=ALU.subtract),
                     reads=[mx_t, lo_t], writes=[w0_t])
            else:
                P.op("dve", lambda e: e.memset(lo[:], -1e30), writes=[lo_t])
            dg = acc[:, qt * 128:(qt + 1) * 128]
            P.op("dve", lambda e: e.tensor_tensor(out=dg, in0=dg, in1=cmask[:], op=ALU.min), reads=[acc_t, cmask_t], writes=[acc_t])
            if nk > TOPK:
                for it in range(NIT):
                    P.op("dve", lambda e: e.tensor_scalar(out=hk[:], in0=w0[:], scalar1=float(2.0 ** -(it + 1)), scalar2=None, op0=ALU.mult),
                         reads=[w0_t], writes=[hk_t])
                    P.op("dve", lambda e: e.tensor_tensor(out=mid[:], in0=lo[:], in1=hk[:], op=ALU.add), reads=[lo_t, hk_t], writes=[mid_t])
                    P.op("pool", lambda e: e.memset(cnt[:], 0.0), writes=[cnt_t])
                    P.op("dve", lambda e: e.tensor_scalar(out=mq[:, :nk], in0=acc[:, :nk], scalar1=mid[:, 0:1], scalar2=0.0, op0=ALU.is_ge, op1=ALU.add,
                                                          accum_out=cnt[:]), reads=[acc_t, mid_t, cnt_t], writes=[mq_t, cnt_t])
                    P.op("dve", lambda e: e.scalar_tensor_tensor(out=tmp[:], in0=cnt[:], scalar=TOPK - 0.5, in1=hk[:], op0=ALU.is_ge, op1=ALU.mult),
                         reads=[cnt_t, hk_t], writes=[tmp_t])
                    P.op("dve", lambda e: e.tensor_tensor(out=lo[:], in0=lo[:], in1=tmp[:], op=ALU.add), reads=[lo_t, tmp_t], writes=[lo_t])
            P.op("dve", lambda e: e.tensor_scalar(out=mq[:, :nk], in0=acc[:, :nk], scalar1=lo[:, 0:1], scalar2=None, op0=ALU.is_ge),
                 reads=[acc_t, lo_t], writes=[mq_t])
            for kb0 in range(0, qt + 1, 4):
                nb = min(4, qt + 1 - kb0)
                j = tri_i[0] % 2
                tri_i[0] += 1
                for b in range(nb):
                    kb = kb0 + b
                    P.op("pe", lambda e: e.transpose(ptr[j][:, b * 128:(b + 1) * 128], mq[:, kb * 128:(kb + 1) * 128], idn[:]),
                         reads=[mq_t, idn_t], writes=[ptr_t[j]])
                src = ptr[j][:, :nb * 128].rearrange("p (b q) -> p b q", q=128)
                P.op("act", lambda e: e.activation(out=maskT[:, kb0:kb0 + nb, qsub * 128:(qsub + 1) * 128], in_=src, func=AF.Copy),
                     reads=[ptr_t[j]], writes=[maskT_t])
            if qt < kmax:
                P.op("pool", lambda e: e.memset(maskT[:, qt + 1:kmax + 1, qsub * 128:(qsub + 1) * 128], 0.0), writes=[maskT_t])
        nkg = (kmax + 1) * 128
        for h in range(NHC):
            kt, kt_t, ksem = ks.next()
            vt, vt_t, vsem = vs.next()
            qt_, qt_t, qsem = qs.next()
            P.dma("sp", kt[:, :nkg], kT[h, :, :nkg], ksem, writes=[kt_t])
            P.dma("sp", vt[:, :kmax + 1, :], vv[h, :, :kmax + 1, :], vsem, writes=[vt_t])
            P.dma("sp", qt_[:], qT[h, :, q0g:q0g + 512], qsem, writes=[qt_t])
            for kb in range(kmax + 1):
                z, z_t = pz.next()
                P.op("pe", lambda e: e.matmul(z[:], kt[:, kb * 128:(kb + 1) * 128], qt_[:], start=True, stop=True), reads=[kt_t, qt_t], writes=[z_t])
                p_, p_t = pr_.next()
                P.op("act", lambda e: e.activation(out=p_[:], in_=z[:], func=AF.Exp, scale=scale), reads=[z_t], writes=[p_t])
                pm, pm_t = pmr.next()
                P.op("pool", lambda e: e.tensor_tensor(out=pm[:], in0=p_[:], in1=maskT[:, kb, :], op=ALU.mult), reads=[p_t, maskT_t], writes=[pm_t])
                P.op("pe", lambda e: e.matmul(pacc[:], vt[:, kb, :], pm[:], start=(kb == 0), stop=(kb == kmax)), reads=[vt_t, pm_t], writes=[pacc_t])
                P.op("pe", lambda e: e.matmul(pden[:], ones[:], pm[:], start=(kb == 0), stop=(kb == kmax)), reads=[ones_t, pm_t], writes=[pden_t])
            P.op("dve", lambda e: e.reciprocal(out=rd[:], in_=pden[:]), reads=[pden_t], writes=[rd_t])
            ot, ot_t, osem = ost.next()
            P.op("dve", lambda e: e.tensor_tensor(out=ot[:], in0=pacc[:], in1=rd[:], op=ALU.mult), reads=[pacc_t, rd_t], writes=[ot_t])
            P.dma("sp", oT[h * 128:(h + 1) * 128, q0g:q0g + 512], ot[:], osem, reads=[ot_t])
    return P.finish(ost.trk)


def build_s5(Sq):
    P = Prog()
    NP_, NCT = 32, 8
    NSC = Sq // 512
    uT = P.dram("uT", [NCT * 128, Sq], F32, "ExternalInput")
    prm_d = P.dram("prm", [128, 3, NP_], F32, "ExternalInput")
    Bt_d = P.dram("Bt", [2, NP_, 128, 128], F32, "ExternalInput")
    Ct_d = P.dram("Ct", [2, NP_, 128, 128], F32, "ExternalInput")
    dv_d = P.dram("dv", [128, NCT], F32, "ExternalInput")
    tl_d = P.dram("tl", [128, 128], F32, "ExternalInput")
    zT = P.dram("zT", [NCT * 128, Sq], BF16, "ExternalOutput")

    s0 = P.dsem()
    prm = P.sb([128, 3, NP_], F32, "prms"); prm_t = Trk()
    P.dma("sp", prm[:], prm_d, s0, writes=[prm_t])
    dv = P.sb([128, NCT], F32, "dvs"); dv_t = Trk()
    P.dma("sp", dv[:], dv_d, s0, writes=[dv_t])
    tl = P.sb([128, 128], F32, "tls"); tl_t = Trk()
    P.dma("sp", tl[:], tl_d, s0, writes=[tl_t])
    onesF = P.sb([128, 128], F32, "onesF"); onesF_t = Trk()
    P.op("dve", lambda e: e.memset(onesF[:], 1.0), writes=[onesF_t])

    def small(name):
        return P.sb([128, NP_], F32, name), Trk()

    lr, li, ldt = prm[:, 0, :], prm[:, 1, :], prm[:, 2, :]
    (dt, dt_t), (th, th_t), (rho, rho_t), (cs, cs_t), (sn, sn_t) = small("dt"), small("th"), small("rho"), small("cs"), small("sn")
    (ar, ar_t), (ai, ai_t), (den, den_t), (am1, am1_t) = small("ar"), small("ai"), small("den"), small("am1")
    (fr, fr_t), (fi, fi_t), (nfr, nfr_t), (t1s, t1s_t) = small("fr"), small("fi"), small("nfr"), small("t1s")
    rf = Ring(P, 3, [128, 128], F32, "rf")
    ri = Ring(P, 2, [128, 128], I32, "rii")
    P.op("act", lambda e: e.activation(out=dt[:], in_=ldt, func=AF.Exp), reads=[prm_t], writes=[dt_t])
    P.op("dve", lambda e: e.tensor_tensor(out=th[:], in0=li, in1=dt[:], op=ALU.mult), reads=[prm_t, dt_t], writes=[th_t])
    P.op("dve", lambda e: e.tensor_tensor(out=rho[:], in0=lr, in1=dt[:], op=ALU.mult), reads=[prm_t, dt_t], writes=[rho_t])
    P.op("act", lambda e: e.activation(out=rho[:], in_=rho[:], func=AF.Exp), reads=[rho_t], writes=[rho_t])
    emit_sin(P, cs[:], cs_t, th[:], th_t, [128, NP_], rf, ri, shift=float(np.pi / 2))
    emit_sin(P, sn[:], sn_t, th[:], th_t, [128, NP_], rf, ri)
    P.op("dve", lambda e: e.tensor_tensor(out=ar[:], in0=rho[:], in1=cs[:], op=ALU.mult), reads=[rho_t, cs_t], writes=[ar_t])
    P.op("dve", lambda e: e.tensor_tensor(out=ai[:], in0=rho[:], in1=sn[:], op=ALU.mult), reads=[rho_t, sn_t], writes=[ai_t])
    P.op("dve", lambda e: e.tensor_tensor(out=den[:], in0=lr, in1=lr, op=ALU.mult), reads=[prm_t], writes=[den_t])
    P.op("dve", lambda e: e.tensor_tensor(out=t1s[:], in0=li, in1=li, op=ALU.mult), reads=[prm_t], writes=[t1s_t])
    P.op("dve", lambda e: e.tensor_tensor(out=den[:], in0=den[:], in1=t1s[:], op=ALU.add), reads=[den_t, t1s_t], writes=[den_t])
    P.op("dve", lambda e: e.reciprocal(out=den[:], in_=den[:]), reads=[den_t], writes=[den_t])
    P.op("dve", lambda e: e.tensor_scalar(out=am1[:], in0=ar[:], scalar1=-1.0, scalar2=None, op0=ALU.add), reads=[ar_t], writes=[am1_t])
    P.op("dve", lambda e: e.tensor_tensor(out=fr[:], in0=am1[:], in1=lr, op=ALU.mult), reads=[am1_t, prm_t], writes=[fr_t])
    P.op("dve", lambda e: e.tensor_tensor(out=t1s[:], in0=ai[:], in1=li, op=ALU.mult), reads=[ai_t, prm_t], writes=[t1s_t])
    P.op("dve", lambda e: e.tensor_tensor(out=fr[:], in0=fr[:], in1=t1s[:], op=ALU.add), reads=[fr_t, t1s_t], writes=[fr_t])
    P.op("dve", lambda e: e.tensor_tensor(out=fr[:], in0=fr[:], in1=den[:], op=ALU.mult), reads=[fr_t, den_t], writes=[fr_t])
    P.op("dve", lambda e: e.tensor_tensor(out=fi[:], in0=ai[:], in1=lr, op=ALU.mult), reads=[ai_t, prm_t], writes=[fi_t])
    P.op("dve", lambda e: e.tensor_tensor(out=t1s[:], in0=am1[:], in1=li, op=ALU.mult), reads=[am1_t, prm_t], writes=[t1s_t])
    P.op("dve", lambda e: e.tensor_tensor(out=fi[:], in0=fi[:], in1=t1s[:], op=ALU.subtract), reads=[fi_t, t1s_t], writes=[fi_t])
    P.op("dve", lambda e: e.tensor_tensor(out=fi[:], in0=fi[:], in1=den[:], op=ALU.mult), reads=[fi_t, den_t], writes=[fi_t])
    P.op("dve", lambda e: e.tensor_scalar(out=nfr[:], in0=fr[:], scalar1=-1.0, scalar2=None, op0=ALU.mult), reads=[fr_t], writes=[nfr_t])
    ar3 = P.sb([128, NP_, 1], F32, "ar3"); ai3 = P.sb([128, NP_, 1], F32, "ai3"); a3_t = Trk()
    P.op("dve", lambda e: e.tensor_copy(out=ar3[:, :, 0], in_=ar[:]), reads=[ar_t], writes=[a3_t])
    P.op("dve", lambda e: e.tensor_copy(out=ai3[:, :, 0], in_=ai[:]), reads=[ai_t], writes=[a3_t])

    cosT = P.sb([128, NP_, 128], F32, "cosT"); sinT = P.sb([128, NP_, 128], F32, "sinT")
    GrT = P.sb([128, NP_, 128], F32, "GrT"); GiT = P.sb([128, NP_, 128], F32, "GiT")
    rhoT = P.sb([128, NP_, 128], F32, "rhoT")
    tab_t = Trk()
    ang = P.sb([128, 128], F32, "ang"); ang_t = Trk()
    for i in range(NP_):
        P.op("dve", lambda e: e.tensor_scalar(out=ang[:], in0=tl[:], scalar1=th[:, i:i + 1], scalar2=None, op0=ALU.mult), reads=[tl_t, th_t], writes=[ang_t])
        emit_sin(P, cosT[:, i, :], tab_t, ang[:], ang_t, [128, 128], rf, ri, shift=float(np.pi / 2))
        emit_sin(P, sinT[:, i, :], tab_t, ang[:], ang_t, [128, 128], rf, ri)
        P.op("pool", lambda e: e.tensor_scalar(out=GrT[:, i, :], in0=cosT[:, i, :], scalar1=fr[:, i:i + 1], scalar2=None, op0=ALU.mult), reads=[tab_t, fr_t], writes=[tab_t])
        P.op("dve", lambda e: e.scalar_tensor_tensor(out=GrT[:, i, :], in0=sinT[:, i, :], scalar=fi[:, i:i + 1], in1=GrT[:, i, :], op0=ALU.mult, op1=ALU.add), reads=[tab_t, fi_t], writes=[tab_t])
        P.op("pool", lambda e: e.tensor_scalar(out=GiT[:, i, :], in0=cosT[:, i, :], scalar1=fi[:, i:i + 1], scalar2=None, op0=ALU.mult), reads=[tab_t, fi_t], writes=[tab_t])
        P.op("dve", lambda e: e.scalar_tensor_tensor(out=GiT[:, i, :], in0=sinT[:, i, :], scalar=nfr[:, i:i + 1], in1=GiT[:, i, :], op0=ALU.mult, op1=ALU.add), reads=[tab_t, nfr_t], writes=[tab_t])
        P.op("pool", lambda e: e.tensor_scalar(out=rhoT[:, i, :], in0=onesF[:], scalar1=rho[:, i:i + 1], scalar2=None, op0=ALU.mult), reads=[onesF_t, rho_t], writes=[tab_t])
    P.op("pool", lambda e: e.memset(rhoT[:, :, 0:1], 0.0), writes=[tab_t])

    Bb = P.sb([128, 2, NP_, 128], BF16, "Bb"); Cb = P.sb([128, 2, NP_, 128], BF16, "Cb"); bc_t = Trk()
    stg = Slots(P, 1, [128, NP_, 128], F32, "stg")
    for (dst, src) in ((Bb, Bt_d), (Cb, Ct_d)):
        for ri_ in range(2):
            st, st_t, ssem = stg.next()
            P.dma("sp", st[:], src[ri_].rearrange("i k m -> k i m"), ssem, writes=[st_t])
            P.op("act", lambda e: e.activation(out=dst[:, ri_, :, :], in_=st[:], func=AF.Copy), reads=[st_t], writes=[bc_t])

    hpr = P.sb([128, NP_, 1], F32, "hpr"); hpi = P.sb([128, NP_, 1], F32, "hpi")
    hp_t = [Trk() for _ in range(NCT)]
    P.op("dve", lambda e: e.memset(hpr[:], 0.0), writes=hp_t)
    P.op("dve", lambda e: e.memset(hpi[:], 0.0), writes=hp_t)
    us = [Slots(P, 1, [128, 512], F32, f"us{c}_") for c in range(NCT)]
    ub = [Ring(P, 1, [128, 512], BF16, f"ub{c}_") for c in range(NCT)]
    wk = {n: Ring(P, 2, [128, 4, 128], F32, "wk_" + n) for n in ("t1", "t2", "br", "bi", "hr", "hi")}
    hb = {n: Ring(P, 2, [128, 4, 128], BF16, "hb_" + n) for n in ("r", "i")}
    c4 = {n: Ring(P, 2, [128, 4, 1], F32, "c4_" + n) for n in ("a", "b", "c", "d")}
    ep = {n: Ring(P, 2, [128, 512], F32, "ep_" + n) for n in ("y", "w")}
    ost = Slots(P, 2, [128, 512], BF16, "ost")
    pP = [P.ps([128, 4, 128], F32, name=f"pP{i}") for i in range(4)]
    pP_t = [Trk() for _ in range(4)]
    py = [P.ps([128, 512], F32, name=f"py{i}") for i in range(4)]
    py_t = [Trk() for _ in range(4)]
    pi_ = [0]

    for sc in range(NSC):
        cur = []
        for ct in range(NCT):
            ut, ut_t, usem = us[ct].next()
            P.dma("sp", ut[:], uT[ct * 128:(ct + 1) * 128, sc * 512:(sc + 1) * 512], usem, writes=[ut_t])
            ubt, ubt_t = ub[ct].next()
            P.op("act", lambda e: e.activation(out=ubt[:], in_=ut[:], func=AF.Copy), reads=[ut_t], writes=[ubt_t])
            cur.append((ut, ut_t, ubt, ubt_t))
        for cth in range(2):
            for n in range(4):
                for cl in range(4):
                    ct = cth * 4 + cl
                    ut, ut_t, ubt, ubt_t = cur[ct]
                    sl4 = slice(ct * 4, ct * 4 + 4)
                    j = pi_[0] % 2
                    pi_[0] += 1
                    Pr, Pr_t, Pi, Pi_t = pP[2 * j], pP_t[2 * j], pP[2 * j + 1], pP_t[2 * j + 1]
                    for pit in range(4):
                        i = ct * 4 + pit
                        P.op("pe", lambda e: e.matmul(Pr[:, pit, :], Bb[:, 0, i, :], ubt[:, n * 128:(n + 1) * 128], start=True, stop=True),
                             reads=[bc_t, ubt_t], writes=[Pr_t])
                        P.op("pe", lambda e: e.matmul(Pi[:, pit, :], Bb[:, 1, i, :], ubt[:, n * 128:(n + 1) * 128], start=True, stop=True),
                             reads=[bc_t, ubt_t], writes=[Pi_t])
                    (t1, t1_t), (t2, t2_t) = wk["t1"].next(), wk["t2"].next()
                    (br, br_t), (bi, bi_t) = wk["br"].next(), wk["bi"].next()
                    G4r, G4i = GrT[:, sl4, :], GiT[:, sl4, :]
                    P.op("dve", lambda e: e.tensor_tensor(out=t1[:], in0=Pr[:], in1=G4r, op=ALU.mult), reads=[Pr_t, tab_t], writes=[t1_t])
                    P.op("dve", lambda e: e.tensor_tensor(out=t2[:], in0=Pi[:], in1=G4i, op=ALU.mult), reads=[Pi_t, tab_t], writes=[t2_t])
                    P.op("pool", lambda e: e.tensor_tensor(out=br[:], in0=t1[:], in1=t2[:], op=ALU.subtract), reads=[t1_t, t2_t], writes=[br_t])
                    (t3, t3_t), (t4, t4_t) = wk["t1"].next(), wk["t2"].next()
                    P.op("dve", lambda e: e.tensor_tensor(out=t3[:], in0=Pi[:], in1=G4r, op=ALU.mult), reads=[Pi_t, tab_t], writes=[t3_t])
                    P.op("dve", lambda e: e.tensor_tensor(out=t4[:], in0=Pr[:], in1=G4i, op=ALU.mult), reads=[Pr_t, tab_t], writes=[t4_t])
                    P.op("pool", lambda e: e.tensor_tensor(out=bi[:], in0=t3[:], in1=t4[:], op=ALU.add), reads=[t3_t, t4_t], writes=[bi_t])
                    (ca, ca_t), (cb, cb_t), (cc, cc_t), (cd, cd_t) = c4["a"].next(), c4["b"].next(), c4["c"].next(), c4["d"].next()
                    a4r, a4i, h4r, h4i = ar3[:, sl4, :], ai3[:, sl4, :], hpr[:, sl4, :], hpi[:, sl4, :]
                    hpt = hp_t[ct]
                    P.op("dve", lambda e: e.tensor_tensor(out=ca[:], in0=a4r, in1=h4r, op=ALU.mult), reads=[a3_t, hpt], writes=[ca_t])
                    P.op("dve", lambda e: e.tensor_tensor(out=cb[:], in0=a4i, in1=h4i, op=ALU.mult), reads=[a3_t, hpt], writes=[cb_t])
                    P.op("dve", lambda e: e.tensor_tensor(out=ca[:], in0=ca[:], in1=cb[:], op=ALU.subtract), reads=[ca_t, cb_t], writes=[ca_t])
                    P.op("dve", lambda e: e.tensor_tensor(out=cc[:], in0=a4r, in1=h4i, op=ALU.mult), reads=[a3_t, hpt], writes=[cc_t])
                    P.op("dve", lambda e: e.tensor_tensor(out=cd[:], in0=a4i, in1=h4r, op=ALU.mult), reads=[a3_t, hpt], writes=[cd_t])
                    P.op("dve", lambda e: e.tensor_tensor(out=cc[:], in0=cc[:], in1=cd[:], op=ALU.add), reads=[cc_t, cd_t], writes=[cc_t])
                    P.op("dve", lambda e: e.tensor_tensor(out=br[:, :, 0:1], in0=br[:, :, 0:1], in1=ca[:], op=ALU.add), reads=[br_t, ca_t], writes=[br_t])
                    P.op("dve", lambda e: e.tensor_tensor(out=bi[:, :, 0:1], in0=bi[:, :, 0:1], in1=cc[:], op=ALU.add), reads=[bi_t, cc_t], writes=[bi_t])
                    (hr, hr_t), (hi, hi_t) = wk["hr"].next(), wk["hi"].next()
                    R4 = rhoT[:, sl4, :].rearrange("p a b -> p (a b)")
                    fl = lambda t: t[:].rearrange("p a b -> p (a b)")
                    P.op("dve", lambda e: e.tensor_tensor_scan(out=fl(hr), data0=R4, data1=fl(br), initial=0.0, op0=ALU.mult, op1=ALU.add),
                         reads=[tab_t, br_t], writes=[hr_t])
                    P.op("dve", lambda e: e.tensor_tensor_scan(out=fl(hi), data0=R4, data1=fl(bi), initial=0.0, op0=ALU.mult, op1=ALU.add),
                         reads=[tab_t, bi_t], writes=[hi_t])
                    cl_, sl_ = cosT[:, sl4, 127:128], sinT[:, sl4, 127:128]
                    (ca, ca_t), (cb, cb_t), (cc, cc_t), (cd, cd_t) = c4["a"].next(), c4["b"].next(), c4["c"].next(), c4["d"].next()
                    P.op("pool", lambda e: e.tensor_tensor(out=ca[:], in0=hr[:, :, 127:128], in1=cl_, op=ALU.mult), reads=[hr_t, tab_t], writes=[ca_t])
                    P.op("pool", lambda e: e.tensor_tensor(out=cb[:], in0=hi[:, :, 127:128], in1=sl_, op=ALU.mult), reads=[hi_t, tab_t], writes=[cb_t])
                    P.op("pool", lambda e: e.tensor_tensor(out=h4r, in0=ca[:], in1=cb[:], op=ALU.subtract), reads=[ca_t, cb_t], writes=[hpt])
                    P.op("pool", lambda e: e.tensor_tensor(out=cc[:], in0=hr[:, :, 127:128], in1=sl_, op=ALU.mult), reads=[hr_t, tab_t], writes=[cc_t])
                    P.op("pool", lambda e: e.tensor_tensor(out=cd[:], in0=hi[:, :, 127:128], in1=cl_, op=ALU.mult), reads=[hi_t, tab_t], writes=[cd_t])
                    P.op("pool", lambda e: e.tensor_tensor(out=h4i, in0=cc[:], in1=cd[:], op=ALU.add), reads=[cc_t, cd_t], writes=[hpt])
                    (u1, u1_t), (u2, u2_t) = wk["br"].next(), wk["bi"].next()
                    (hrb, hrb_t), (hib, hib_t) = hb["r"].next(), hb["i"].next()
                    C4, S4 = cosT[:, sl4, :], sinT[:, sl4, :]
                    P.op("pool", lambda e: e.tensor_tensor(out=u1[:], in0=hr[:], in1=C4, op=ALU.mult), reads=[hr_t, tab_t], writes=[u1_t])
                    P.op("pool", lambda e: e.tensor_tensor(out=u2[:], in0=hi[:], in1=S4, op=ALU.mult), reads=[hi_t, tab_t], writes=[u2_t])
                    P.op("pool", lambda e: e.tensor_tensor(out=hrb[:], in0=u1[:], in1=u2[:], op=ALU.subtract), reads=[u1_t, u2_t], writes=[hrb_t])
                    P.op("pool", lambda e: e.tensor_tensor(out=u1[:], in0=hr[:], in1=S4, op=ALU.mult), reads=[hr_t, tab_t, hrb_t], writes=[u1_t])
                    P.op("pool", lambda e: e.tensor_tensor(out=u2[:], in0=hi[:], in1=C4, op=ALU.mult), reads=[hi_t, tab_t, hrb_t], writes=[u2_t])
                    P.op("dve", lambda e: e.scalar_tensor_tensor(out=hib[:], in0=u1[:], scalar=-1.0, in1=u2[:], op0=ALU.mult, op1=ALU.subtract),
                         reads=[u1_t, u2_t], writes=[hib_t])
                    for pit in range(4):
                        i = ct * 4 + pit
                        P.op("pe", lambda e: e.matmul(py[cl][:, n * 128:(n + 1) * 128], Cb[:, 0, i, :], hrb[:, pit, :], start=(pit == 0), stop=False),
                             reads=[bc_t, hrb_t], writes=[py_t[cl]])
                        P.op("pe", lambda e: e.matmul(py[cl][:, n * 128:(n + 1) * 128], Cb[:, 1, i, :], hib[:, pit, :], start=False, stop=(pit == 3)),
                             reads=[bc_t, hib_t], writes=[py_t[cl]])
            for cl in range(4):
                ct = cth * 4 + cl
                ut, ut_t, ubt, ubt_t = cur[ct]
                (y2, y2_t), (w_, w_t) = ep["y"].next(), ep["w"].next()
                P.op("dve", lambda e: e.scalar_tensor_tensor(out=y2[:], in0=ut[:], scalar=dv[:, ct:ct + 1], in1=py[cl][:], op0=ALU.mult, op1=ALU.add),
                     reads=[ut_t, dv_t, py_t[cl]], writes=[y2_t])
                P.op("pool", lambda e: e.tensor_tensor(out=w_[:], in0=y2[:], in1=y2[:], op=ALU.mult), reads=[y2_t], writes=[w_t])
                P.op("pool", lambda e: e.tensor_scalar(out=w_[:], in0=w_[:], scalar1=0.044715, scalar2=1.0, op0=ALU.mult, op1=ALU.add), reads=[w_t], writes=[w_t])
                P.op("pool", lambda e: e.tensor_tensor(out=w_[:], in0=w_[:], in1=y2[:], op=ALU.mult), reads=[w_t, y2_t], writes=[w_t])
                P.op("act", lambda e: e.activation(out=w_[:], in_=w_[:], func=AF.Sigmoid, scale=1.5957691216057308), reads=[w_t], writes=[w_t])
                ot, ot_t, osem = ost.next()
                P.op("dve", lambda e: e.tensor_tensor(out=ot[:], in0=y2[:], in1=w_[:], op=ALU.mult), reads=[y2_t, w_t], writes=[ot_t])
                P.dma("sp", zT[ct * 128:(ct + 1) * 128, sc * 512:(sc + 1) * 512], ot[:], osem, reads=[ot_t])
    return P.finish(ost.trk)


def s5_layout(lam_re, lam_im, log_dt, b_re, b_im, c_re, c_im, gsl):
    G = 64
    lr = lam_re[gsl].reshape(32, 128).T
    li = lam_im[gsl].reshape(32, 128).T
    ld = np.repeat(log_dt[gsl].reshape(32, 2, 1), 64, axis=2).reshape(32, 128).T
    prm = np.ascontiguousarray(np.stack([lr, li, ld], axis=1)).astype(np.float32)
    Bt = np.zeros((2, 32, 128, 128), np.float32)
    Ct = np.zeros((2, 32, 128, 128), np.float32)
    for gi in range(G):
        g = gsl.start + gi
        pair, g2 = gi // 2, gi % 2
        pit = pair % 4
        ks = slice(pit * 32 + g2 * 16, pit * 32 + g2 * 16 + 16)
        ms = slice(g2 * 64, g2 * 64 + 64)
        Bt[0, pair, ks, ms] = b_re[g].T
        Bt[1, pair, ks, ms] = b_im[g].T
        Ct[0, pair, ms, ks] = c_re[g].T
        Ct[1, pair, ms, ks] = c_im[g].T
    return prm, Bt, Ct


CW = 2048


def build_cast(R):
    P = Prog()
    wi_ = P.dram("wflat", [R, 128, CW], F32, "ExternalInput")
    wo_ = P.dram("wb", [R, 128, CW], BF16, "ExternalOutput")
    ins = Slots(P, 3, [128, CW], F32, "cin")
    outs = Slots(P, 3, [128, CW], BF16, "cout")
    for r in range(R):
        it, it_t, isem = ins.next()
        P.dma("sp", it[:], wi_[r], isem, writes=[it_t])
        ot, ot_t, osem = outs.next()
        eng = ("act", "dve", "pool")[r % 3]
        if eng == "act":
            P.op("act", lambda e: e.activation(out=ot[:], in_=it[:], func=AF.Copy), reads=[it_t], writes=[ot_t])
        else:
            P.op(eng, lambda e: e.tensor_copy(out=ot[:], in_=it[:]), reads=[it_t], writes=[ot_t])
        P.dma("sp", wo_[r], ot[:], osem, reads=[ot_t])
    return P.finish(outs.trk)


def build_ada(NL, NCOL):
    P = Prog()
    NBK = NCOL // 512
    cT = P.dram("cT", [128, DC, B], F32, "ExternalInput")
    w = P.dram("w", [NL, D, NCOL], F32, "ExternalInput")
    bias = P.dram("bias", [NL, B, NCOL], F32, "ExternalInput")
    mod = P.dram("mod", [NL, B, NCOL], F32, "ExternalOutput")
    s0 = P.dsem()
    ct = P.sb([128, DC, B], F32, "cts"); ct_t = Trk()
    P.dma("sp", ct[:], cT, s0, writes=[ct_t])
    cond = P.sb([128, DC, B], F32, "cond"); cond_t = Trk()
    P.op("act", lambda e: e.activation(out=cond[:], in_=ct[:], func=AF.Silu), reads=[ct_t], writes=[cond_t])
    ws = Slots(P, 4, [128, NCOL], F32, "aw")
    bs = Slots(P, 2, [B, NCOL], F32, "ab")
    os_ = Slots(P, 2, [B, NCOL], F32, "ao")
    ps = [P.ps([128, 512], F32, name=f"aps{i}") for i in range(NBK)]
    ps_t = [Trk() for _ in range(NBK)]
    for l in range(NL):
        bt, bt_t, bsem = bs.next()
        P.dma("sp", bt[:], bias[l], bsem, writes=[bt_t])
        for k in range(DC):
            wt, wt_t, wsem = ws.next()
            P.dma("sp", wt[:], w[l, k * 128:(k + 1) * 128, :], wsem, writes=[wt_t])
            for nb in range(NBK):
                P.op("pe", lambda e: e.matmul(ps[nb][0:B, :], cond[:, k, :], wt[:, nb * 512:(nb + 1) * 512], start=(k == 0), stop=(k == DC - 1)),
                     reads=[cond_t, wt_t], writes=[ps_t[nb]])
        ot, ot_t, osem = os_.next()
        for nb in range(NBK):
            P.op("dve", lambda e: e.tensor_tensor(out=ot[:, nb * 512:(nb + 1) * 512], in0=ps[nb][0:B, :], in1=bt[:, nb * 512:(nb + 1) * 512], op=ALU.add),
                 reads=[ps_t[nb], bt_t], writes=[ot_t])
        P.dma("sp", mod[l], ot[:], osem, reads=[ot_t])
    return P.finish(os_.trk)


_CACHE = {}


def _prog(key, fn):
    if key not in _CACHE:
        _CACHE[key] = fn()
    return _CACHE[key]


def _run(nc, in_maps):
    res = run_bass_kernel_spmd(nc, in_maps, core_ids=list(range(NCORES)))
    return res.results


def _c(a):
    return np.ascontiguousarray(a)


def kernel(x, c, positions, ln1_g, ln2_g, ada_w, ada_b, mlp_w1, mlp_w2,
           sb_w_in, sb_q_gain, sb_k_gain, sb_w_out,
           s5_w_in, s5_lambda_re, s5_lambda_im, s5_log_dt, s5_b_re, s5_b_im,
           s5_c_re, s5_c_im, s5_d, s5_w_glu,
           dsa_w_in, dsa_q_gain, dsa_k_gain, dsa_w_out):
    f32 = lambda a: np.asarray(a, dtype=np.float32)
    x, c = f32(x), f32(c)
    positions = np.asarray(positions).astype(np.int32)
    NL = ln1_g.shape[0]
    TH = S // 2
    KB = S // 128

    names = ["mlp_w1", "mlp_w2", "sb_w_in", "sb_w_out", "s5_w_in", "s5_w_glu", "dsa_w_in", "dsa_w_out"]
    arrs = [f32(a) for a in (mlp_w1, mlp_w2, sb_w_in, sb_w_out, s5_w_in, s5_w_glu, dsa_w_in, dsa_w_out)]
    sizes = [a.size for a in arrs]
    tot = sum(sizes)
    per = 128 * CW
    R = -(-tot // (per * NCORES))
    flat = np.zeros(R * per * NCORES, np.float32)
    o = 0
    for a in arrs:
        flat[o:o + a.size] = a.reshape(-1)
        o += a.size
    flat = flat.reshape(NCORES, R, 128, CW)
    nc = _prog(("cast", R), lambda: build_cast(R))
    res = _run(nc, [{"wflat": flat[i]} for i in range(NCORES)])
    fb = np.concatenate([r["wb"].reshape(-1) for r in res])
    del flat
    wb = {}
    o = 0
    for n, a in zip(names, arrs):
        wb[n] = fb[o:o + a.size].reshape(a.shape)
        o += a.size
    del fb, arrs

    NCOL = 6 * D // NCORES
    cT = _c(c.T.reshape(DC, 128, B).transpose(1, 0, 2))
    ada_w, ada_b = f32(ada_w), f32(ada_b)
    nc = _prog(("ada", NL), lambda: build_ada(NL, NCOL))
    ims = []
    for i in range(NCORES):
        sl = slice(i * NCOL, (i + 1) * NCOL)
        ims.append({"cT": cT, "w": _c(ada_w[:, :, sl]), "bias": _c(np.broadcast_to(ada_b[:, None, sl], (NL, B, NCOL)))})
    res = _run(nc, ims)
    mod = np.concatenate([r["mod"] for r in res], axis=2)
    del ims

    xT = _c(x.transpose(0, 2, 1))
    idn = np.eye(128, dtype=np.float32).astype(NPBF)
    fv = rope_consts()
    cores = [(b, j) for b in range(B) for j in range(2)]
    cnt = [0, 0, 0]
    for l in range(NL):
        kind = l % 3
        jm = cnt[kind]
        cnt[kind] += 1
        sh1, sc1, g1, sh2, sc2, g2 = [mod[l][:, i * D:(i + 1) * D] for i in range(6)]
        if kind == 0:
            wf, items, NOUT, wt, gv = proj_plan_sb(wb["sb_w_in"][jm], f32(sb_q_gain[jm]), f32(sb_k_gain[jm]))
            odt, rope_on = BF16, False
        elif kind == 1:
            wf, items, NOUT, wt, gv = proj_plan_s5(wb["s5_w_in"][jm])
            odt, rope_on = F32, False
        else:
            wf, items, NOUT, wt, gv = proj_plan_dsa(wb["dsa_w_in"][jm], f32(dsa_q_gain[jm]), f32(dsa_k_gain[jm]))
            odt, rope_on = BF16, True
        NT = 0 if wt is None else wt.shape[0]
        nc = _prog(("proj", kind), lambda: build_proj(TH, items, wf.shape[0], NOUT, odt, NT, gv.shape[1], rope_on))
        ims = []
        for (b, hf) in cores:
            im = {"xT": _c(xT[b][:, hf * TH:(hf + 1) * TH]),
                  "vec": _c(np.stack([lay_vec(f32(ln1_g[l])), lay_vec(sc1[b]), lay_vec(sh1[b])], axis=1)),
                  "wf": wf, "gv": gv}
            if NT:
                im["wt"] = wt
            if rope_on:
                im["pos"] = _c(positions[b][None, hf * TH:(hf + 1) * TH])
                im["fv"] = fv
            ims.append(im)
        res = _run(nc, ims)
        yT = [np.concatenate([res[2 * b]["yT"], res[2 * b + 1]["yT"]], axis=1) for b in range(B)]
        if NT:
            vo = [np.concatenate([res[2 * b]["vo"], res[2 * b + 1]["vo"]], axis=0) for b in range(B)]
        del ims, res
        if kind in (0, 2):
            ims = []
            for (b, j) in cores:
                hs = slice(8 * j, 8 * j + 8)
                y3 = yT[b].reshape(NOUT, 128, S)
                v4 = vo[b].reshape(KB, 128, NH, 128)[:, :, hs, :].transpose(2, 1, 0, 3)
                im = {"qT": _c(y3[0:NH][hs]), "kT": _c(y3[NH:2 * NH][hs]), "v": _c(v4)}
                if kind == 2:
                    im["qiT"] = _c(y3[2 * NH:2 * NH + 8])
                    im["kiT"] = _c(y3[2 * NH + 8])
                    im["wi"] = _c(y3[2 * NH + 9][:16].reshape(16, KB, 128).transpose(2, 1, 0))
                    im["idn"] = idn
                ims.append(im)
            if kind == 0:
                nc = _prog(("sb",), lambda: build_sb(S, 8))
            else:
                nc = _prog(("dsa",), lambda: build_dsa(S, 8, min(256, S // 4)))
            res = _run(nc, ims)
            oT = [np.concatenate([res[2 * b]["oT"], res[2 * b + 1]["oT"]], axis=0) for b in range(B)]
            wo = lay_w(wb["sb_w_out"][jm] if kind == 0 else wb["dsa_w_out"][jm])
            glu = False
        else:
            ims = []
            lre, lim, ldt = f32(s5_lambda_re[jm]), f32(s5_lambda_im[jm]), f32(s5_log_dt[jm])
            bre, bim, cre, cim = f32(s5_b_re[jm]), f32(s5_b_im[jm]), f32(s5_c_re[jm]), f32(s5_c_im[jm])
            dsk = f32(s5_d[jm])
            tl = _c(np.tile(np.arange(128, dtype=np.float32)[None, :], (128, 1)))
            lay = [s5_layout(lre, lim, ldt, bre, bim, cre, cim, slice(64 * j, 64 * j + 64)) for j in range(2)]
            for (b, j) in cores:
                prm, Bt, Ct = lay[j]
                ims.append({"uT": _c(yT[b][j * 1024:(j + 1) * 1024]), "prm": prm, "Bt": Bt, "Ct": Ct,
                            "dv": lay_vec(dsk[j * 1024:(j + 1) * 1024]), "tl": tl})
            nc = _prog(("s5",), lambda: build_s5(S))
            res = _run(nc, ims)
            oT = [np.concatenate([res[2 * b]["zT"], res[2 * b + 1]["zT"]], axis=0) for b in range(B)]
            wo = lay_w(wb["s5_w_glu"][jm])
            glu = True
        del ims, res, yT
        w1l, w2l = lay_w(wb["mlp_w1"][l]), lay_w(wb["mlp_w2"][l])
        nc = _prog(("mlp", glu), lambda: build_mlp(TH, glu))
        ims = []
        for (b, hf) in cores:
            ts = slice(hf * TH, (hf + 1) * TH)
            vec = np.stack([lay_vec(g1[b]), lay_vec(f32(ln2_g[l])), lay_vec(sc2[b]), lay_vec(sh2[b]), lay_vec(g2[b])], axis=1)
            ims.append({"xT": _c(xT[b][:, ts]), "oT": _c(oT[b][:, ts]), "wo": wo, "w1": w1l, "w2": w2l, "vec": _c(vec)})
        res = _run(nc, ims)
        xT = np.stack([np.concatenate([res[2 * b]["yT"], res[2 * b + 1]["yT"]], axis=1) for b in range(B)])
        del ims, res, oT
    return _c(xT.transpose(0, 2, 1)).astype(np.float32)
```

```python
import contextlib
import numpy as np
import ml_dtypes
import concourse.bass as bass
import concourse.mybir as mybir
from concourse.bass_utils import run_bass_kernel_spmd

F32 = mybir.dt.float32
BF16 = mybir.dt.bfloat16
AF = mybir.ActivationFunctionType
ALU = mybir.AluOpType
AX = mybir.AxisListType
NPBF = ml_dtypes.bfloat16

D = 2048
DC = 16
DFF = 8192
FC = 64
B = 4
S = 4096
NH = 16
HD = 128
EPS = 1e-6
NCORES = 8


class Trk:
    __slots__ = ("w", "r")

    def __init__(self):
        self.w = None
        self.r = {}


class Prog:
    def __init__(self):
        self.nc = bass.Bass("TRN2", target_bir_lowering=False)
        nc = self.nc
        self.es = contextlib.ExitStack()
        self.eng = {"pe": nc.tensor, "act": nc.scalar, "dve": nc.vector, "pool": nc.gpsimd, "sp": nc.sync}
        self.sems = {}
        self.cnt = {}
        self.known = {}
        for e in self.eng:
            self.sems[e] = self.es.enter_context(nc.semaphore("s_" + e))
            self.cnt[e] = 0
            self.known[e] = {}
        self.ndma = 0
        self.uid = 0

    def sb(self, shape, dt, name=None):
        self.uid += 1
        return self.es.enter_context(self.nc.sbuf_tensor(name or f"sb{self.uid}", list(shape), dt))

    def ps(self, shape=(128, 512), dt=F32, name=None):
        self.uid += 1
        return self.es.enter_context(self.nc.psum_tensor(name or f"ps{self.uid}", list(shape), dt))

    def dram(self, name, shape, dt, kind):
        return self.nc.dram_tensor(name, list(shape), dt, kind=kind).ap()

    def dsem(self):
        self.ndma += 1
        k = f"d{self.ndma}"
        self.sems[k] = self.es.enter_context(self.nc.semaphore(k))
        self.cnt[k] = 0
        return k

    def _waits(self, e, reads, writes):
        need = {}
        for t in reads:
            if t.w is not None:
                k, v = t.w
                if need.get(k, 0) < v:
                    need[k] = v
        for t in writes:
            if t.w is not None:
                k, v = t.w
                if need.get(k, 0) < v:
                    need[k] = v
            for k, v in t.r.items():
                if need.get(k, 0) < v:
                    need[k] = v
        kn = self.known[e]
        eng = self.eng[e]
        for k, v in need.items():
            if k == "pe" and e == "pe":
                continue
            if k not in self.eng:
                v = self.cnt[k]
            if kn.get(k, 0) < v:
                eng.wait_ge(self.sems[k], v)
                kn[k] = v

    def op(self, e, fn, reads=(), writes=()):
        self._waits(e, reads, writes)
        ins = fn(self.eng[e])
        self.cnt[e] += 1
        v = self.cnt[e]
        ins.then_inc(self.sems[e], 1)
        for t in reads:
            t.r[e] = v
        for t in writes:
            t.w = (e, v)
            t.r = {}
        return ins

    def dma(self, q, out, in_, sem, reads=(), writes=()):
        self._waits(q, reads, writes)
        ins = self.eng[q].dma_start(out=out, in_=in_)
        self.cnt[sem] += 16
        v = self.cnt[sem]
        ins.then_inc(self.sems[sem], 16)
        for t in reads:
            t.r[sem] = v
        for t in writes:
            t.w = (sem, v)
            t.r = {}
        return ins

    def finish(self, trks):
        need = {}
        for t in trks:
            if t.w is not None:
                need[t.w[0]] = max(need.get(t.w[0], 0), t.w[1])
            for k, v in t.r.items():
                need[k] = max(need.get(k, 0), v)
        for k, v in need.items():
            self.eng["sp"].wait_ge(self.sems[k], v)
        self.es.close()
        return self.nc


class Slots:
    def __init__(self, P, n, shape, dt, name):
        self.tiles = [P.sb(shape, dt, f"{name}{i}") for i in range(n)]
        self.trk = [Trk() for _ in range(n)]
        self.sem = [P.dsem() for _ in range(n)]
        self.i = 0
        self.n = n

    def next(self):
        j = self.i % self.n
        self.i += 1
        return self.tiles[j], self.trk[j], self.sem[j]


class PsumRing:
    def __init__(self, P, n, name="pr"):
        self.tiles = [P.ps(name=f"{name}{i}") for i in range(n)]
        self.trk = [Trk() for _ in range(n)]
        self.i = 0
        self.n = n

    def next(self):
        j = self.i % self.n
        self.i += 1
        return self.tiles[j], self.trk[j]


def lay_w(w):
    K, N = w.shape
    return np.ascontiguousarray(w.reshape(K // 128, 128, N // 128, 128).transpose(2, 1, 0, 3))


def lay_vec(v):
    return np.ascontiguousarray(v.reshape(-1, 128).T)


def build_mlp(T, glu):
    P = Prog()
    TB = 512
    NB = T // TB
    NO = 32 if glu else 16
    xT = P.dram("xT", [D, T], F32, "ExternalInput")
    oT = P.dram("oT", [D, T], BF16, "ExternalInput")
    wo = P.dram("wo", [NO, 128, DC, 128], BF16, "ExternalInput")
    w1 = P.dram("w1", [FC, 128, DC, 128], BF16, "ExternalInput")
    w2 = P.dram("w2", [DC, 128, FC, 128], BF16, "ExternalInput")
    vec = P.dram("vec", [128, 5, DC], F32, "ExternalInput")
    yT = P.dram("yT", [D, T], F32, "ExternalOutput")
    xTv = xT.rearrange("(c p) t -> p c t", p=128)
    oTv = oT.rearrange("(c p) t -> p c t", p=128)
    yTv = yT.rearrange("(c p) t -> p c t", p=128)

    vt = P.sb([128, 5, DC], F32, "vt")
    vt_t = Trk()
    vsem = P.dsem()
    gm = P.sb([128, DC], F32, "gm")
    gm_t = Trk()
    ones = P.sb([128, 128], BF16, "ones")
    ones_t = Trk()
    P.dma("sp", vt[:], vec, vsem, writes=[vt_t])
    P.op("dve", lambda e: e.memset(ones[:], 1.0), writes=[ones_t])
    P.op("dve", lambda e: e.scalar_tensor_tensor(out=gm[:], in0=vt[:, 2, :], scalar=1.0, in1=vt[:, 1, :],
                                                 op0=ALU.add, op1=ALU.mult), reads=[vt_t], writes=[gm_t])

    xs = Slots(P, 2, [128, DC, TB], F32, "xs")
    os_ = Slots(P, 1, [128, DC, TB], BF16, "os")
    ws16 = Slots(P, 3, [128, DC, 128], BF16, "ws")
    ws64 = Slots(P, 2, [128, FC // 2, 128], BF16, "wl")
    hT = P.sb([128, DC, TB], BF16, "hT")
    hT_t = Trk()
    aT = P.sb([128, FC, TB], BF16, "aT")
    aT_t = [Trk() for _ in range(FC)]
    sq = hT
    sq_t = hT_t
    tmp = [P.sb([128, TB], F32, f"tmp{i}") for i in range(3)]
    tmp_t = [Trk() for _ in range(3)]
    rstd = P.sb([128, TB], F32, "rstd")
    rstd_t = Trk()
    pr = PsumRing(P, 6)
    pstat = P.ps(name="pstat")
    pstat_t = Trk()
    ti = [0]

    def nexttmp():
        j = ti[0] % 3
        ti[0] += 1
        return tmp[j], tmp_t[j]

    for blk in range(NB):
        t0 = blk * TB
        xb, xb_t, xsem = xs.next()
        ob, ob_t, osem = os_.next()
        P.dma("sp", xb[:], xTv[:, :, t0:t0 + TB], xsem, writes=[xb_t])
        P.dma("sp", ob[:], oTv[:, :, t0:t0 + TB], osem, writes=[ob_t])
        if not glu:
            for n in range(DC):
                wt, wt_t, wsem = ws16.next()
                P.dma("sp", wt[:], wo[n], wsem, writes=[wt_t])
                pt, pt_t = pr.next()
                for k in range(DC):
                    P.op("pe", lambda e, k=k: e.matmul(pt[:], wt[:, k, :], ob[:, k, :], start=(k == 0), stop=(k == DC - 1)),
                         reads=[wt_t, ob_t], writes=[pt_t])
                P.op("dve", lambda e: e.scalar_tensor_tensor(out=xb[:, n, :], in0=pt[:], scalar=vt[:, 0, n:n + 1], in1=xb[:, n, :],
                                                             op0=ALU.mult, op1=ALU.add),
                     reads=[pt_t, vt_t], writes=[xb_t])
        else:
            for n in range(DC):
                pts = []
                for half in range(2):
                    wt, wt_t, wsem = ws16.next()
                    P.dma("sp", wt[:], wo[n + DC * half], wsem, writes=[wt_t])
                    pt, pt_t = pr.next()
                    for k in range(DC):
                        P.op("pe", lambda e, k=k: e.matmul(pt[:], wt[:, k, :], ob[:, k, :], start=(k == 0), stop=(k == DC - 1)),
                             reads=[wt_t, ob_t], writes=[pt_t])
                    pts.append((pt, pt_t))
                (pa, pa_t), (pg, pg_t) = pts
                tg, tg_t = nexttmp()
                P.op("act", lambda e: e.activation(out=tg[:], in_=pg[:], func=AF.Sigmoid), reads=[pg_t], writes=[tg_t])
                P.op("dve", lambda e: e.tensor_tensor(out=tg[:], in0=pa[:], in1=tg[:], op=ALU.mult), reads=[pa_t, tg_t], writes=[tg_t])
                P.op("dve", lambda e: e.scalar_tensor_tensor(out=xb[:, n, :], in0=tg[:], scalar=vt[:, 0, n:n + 1], in1=xb[:, n, :],
                                                             op0=ALU.mult, op1=ALU.add),
                     reads=[tg_t, vt_t], writes=[xb_t])
        P.op("act", lambda e: e.activation(out=sq[:], in_=xb[:], func=AF.Square), reads=[xb_t], writes=[sq_t])
        for k in range(DC):
            P.op("pe", lambda e, k=k: e.matmul(pstat[:], ones[:], sq[:, k, :], start=(k == 0), stop=(k == DC - 1)),
                 reads=[ones_t, sq_t], writes=[pstat_t])
        P.op("dve", lambda e: e.tensor_scalar(out=rstd[:], in0=pstat[:], scalar1=1.0 / D, scalar2=EPS, op0=ALU.mult, op1=ALU.add),
             reads=[pstat_t], writes=[rstd_t])
        P.op("act", lambda e: e.activation(out=rstd[:], in_=rstd[:], func=AF.Sqrt), reads=[rstd_t], writes=[rstd_t])
        P.op("dve", lambda e: e.reciprocal(out=rstd[:], in_=rstd[:]), reads=[rstd_t], writes=[rstd_t])
        for c in range(DC):
            tt, tt_t = nexttmp()
            P.op("dve", lambda e: e.tensor_tensor(out=tt[:], in0=xb[:, c, :], in1=rstd[:], op=ALU.mult),
                 reads=[xb_t, rstd_t], writes=[tt_t])
            P.op("act", lambda e: e.activation(out=hT[:, c, :], in_=tt[:], func=AF.Identity, scale=gm[:, c:c + 1], bias=vt[:, 3, c:c + 1]),
                 reads=[tt_t, gm_t, vt_t], writes=[hT_t])
        for n in range(FC):
            wt, wt_t, wsem = ws16.next()
            P.dma("sp", wt[:], w1[n], wsem, writes=[wt_t])
            pt, pt_t = pr.next()
            for k in range(DC):
                P.op("pe", lambda e, k=k: e.matmul(pt[:], wt[:, k, :], hT[:, k, :], start=(k == 0), stop=(k == DC - 1)),
                     reads=[wt_t, hT_t], writes=[pt_t])
            tt, tt_t = nexttmp()
            P.op("act", lambda e: e.activation(out=tt[:], in_=pt[:], func=AF.Relu), reads=[pt_t], writes=[tt_t])
            P.op("dve", lambda e: e.tensor_tensor(out=aT[:, n, :], in0=tt[:], in1=tt[:], op=ALU.mult), reads=[tt_t], writes=[aT_t[n]])
        for n in range(DC):
            pt, pt_t = pr.next()
            for hf in range(2):
                wt, wt_t, wsem = ws64.next()
                P.dma("sp", wt[:], w2[n, :, hf * 32:(hf + 1) * 32, :], wsem, writes=[wt_t])
                for kk in range(FC // 2):
                    k = hf * 32 + kk
                    P.op("pe", lambda e: e.matmul(pt[:], wt[:, kk, :], aT[:, k, :], start=(k == 0), stop=(k == FC - 1)),
                         reads=[wt_t, aT_t[k]], writes=[pt_t])
            P.op("dve", lambda e: e.scalar_tensor_tensor(out=xb[:, n, :], in0=pt[:], scalar=vt[:, 4, n:n + 1], in1=xb[:, n, :],
                                                         op0=ALU.mult, op1=ALU.add),
                 reads=[pt_t, vt_t], writes=[xb_t])
        P.dma("sp", yTv[:, :, t0:t0 + TB], xb[:], xsem, reads=[xb_t])
    return P.finish(xs.trk)


I32 = mybir.dt.int32
TWO_PI = float(2 * np.pi)


class Ring:
    def __init__(self, P, n, shape, dt, name):
        self.tiles = [P.sb(shape, dt, f"{name}{i}") for i in range(n)]
        self.trk = [Trk() for _ in range(n)]
        self.i = 0
        self.n = n

    def next(self):
        j = self.i % self.n
        self.i += 1
        return self.tiles[j], self.trk[j]


def emit_sin(P, out, out_t, ang, ang_t, shape, ring_f, ring_i, shift=0.0, eng="dve"):
    kf, kf_t = ring_f.next()
    ki, ki_t = ring_i.next()
    sl = tuple(slice(0, s) for s in shape[1:])
    kfv = kf[(slice(0, shape[0]),) + sl]
    kiv = ki[(slice(0, shape[0]),) + sl]
    P.op(eng, lambda e: e.tensor_scalar(out=kfv, in0=ang, scalar1=shift, scalar2=1.0 / TWO_PI, op0=ALU.add, op1=ALU.mult),
         reads=[ang_t], writes=[kf_t])
    P.op(eng, lambda e: e.tensor_copy(out=kiv, in_=kfv), reads=[kf_t], writes=[ki_t])
    P.op(eng, lambda e: e.tensor_copy(out=kfv, in_=kiv), reads=[ki_t], writes=[kf_t])
    P.op(eng, lambda e: e.scalar_tensor_tensor(out=kfv, in0=kfv, scalar=-TWO_PI, in1=ang, op0=ALU.mult, op1=ALU.add),
         reads=[kf_t, ang_t], writes=[kf_t])
    if shift != 0.0:
        P.op(eng, lambda e: e.tensor_scalar(out=kfv, in0=kfv, scalar1=shift, scalar2=None, op0=ALU.add), reads=[kf_t], writes=[kf_t])
    P.op(eng, lambda e: e.tensor_scalar(out=kfv, in0=kfv, scalar1=float(np.pi), scalar2=-float(np.pi), op0=ALU.min, op1=ALU.max),
         reads=[kf_t], writes=[kf_t])
    P.op("act", lambda e: e.activation(out=out, in_=kfv, func=AF.Sin), reads=[kf_t], writes=[out_t])


def build_proj(T, items, NWF, NOUT, out_dt, NT, NG, use_rope):
    P = Prog()
    TB = 512
    NB = T // TB
    xT = P.dram("xT", [D, T], F32, "ExternalInput")
    vec = P.dram("vec", [128, 3, DC], F32, "ExternalInput")
    wf = P.dram("wf", [NWF, 128, DC, 128], BF16, "ExternalInput")
    yT = P.dram("yT", [NOUT * 128, T], out_dt, "ExternalOutput")
    if NT:
        wt_d = P.dram("wt", [NT, 128, DC, 512], BF16, "ExternalInput")
        vo = P.dram("vo", [T, NT * 512], BF16, "ExternalOutput")
    gv_d = P.dram("gv", [128, max(NG, 1)], F32, "ExternalInput")
    if use_rope:
        pos = P.dram("pos", [1, T], I32, "ExternalInput")
        fv_d = P.dram("fv", [128, 4], F32, "ExternalInput")
    xTv = xT.rearrange("(c p) t -> p c t", p=128)

    vt = P.sb([128, 3, DC], F32, "vt"); vt_t = Trk()
    gv = P.sb([128, max(NG, 1)], F32, "gvs"); gv_t = Trk()
    gm = P.sb([128, DC], F32, "gm"); gm_t = Trk()
    ones = P.sb([128, 128], BF16, "ones"); ones_t = Trk()
    s0 = P.dsem()
    P.dma("sp", vt[:], vec, s0, writes=[vt_t])
    P.dma("sp", gv[:], gv_d, s0, writes=[gv_t])
    P.op("dve", lambda e: e.memset(ones[:], 1.0), writes=[ones_t])
    P.op("dve", lambda e: e.scalar_tensor_tensor(out=gm[:], in0=vt[:, 1, :], scalar=1.0, in1=vt[:, 0, :],
                                                 op0=ALU.add, op1=ALU.mult), reads=[vt_t], writes=[gm_t])
    tmpr = Ring(P, 4, [128, TB], F32, "tmp")
    tabs = {}
    if use_rope:
        fv = P.sb([128, 4], F32, "fvs"); fv_t = Trk()
        P.dma("sp", fv[:], fv_d, s0, writes=[fv_t])
        posi = P.sb([128, T], I32, "posi"); posi_t = Trk()
        P.dma("sp", posi[:], pos.broadcast_to([128, T]), s0, writes=[posi_t])
        posf = P.sb([128, T], F32, "posf"); posf_t = Trk()
        P.op("dve", lambda e: e.tensor_copy(out=posf[:], in_=posi[:]), reads=[posi_t], writes=[posf_t])
        ring_i = Ring(P, 2, [128, TB], I32, "ri")
        ang = P.sb([128, TB], F32, "ang"); ang_t = Trk()
        for nm, fcol, scol in (("m", 0, 1), ("i", 2, 3)):
            ct = P.sb([128, T], F32, "cos" + nm); ct_t = Trk()
            st = P.sb([128, T], F32, "sin" + nm); st_t = Trk()
            for b0 in range(0, T, TB):
                P.op("dve", lambda e: e.tensor_scalar(out=ang[:], in0=posf[:, b0:b0 + TB], scalar1=fv[:, fcol:fcol + 1], scalar2=None, op0=ALU.mult),
                     reads=[posf_t, fv_t], writes=[ang_t])
                emit_sin(P, ct[:, b0:b0 + TB], ct_t, ang[:], ang_t, [128, TB], tmpr, ring_i, shift=float(np.pi / 2))
                emit_sin(P, st[:, b0:b0 + TB], st_t, ang[:], ang_t, [128, TB], tmpr, ring_i)
                P.op("dve", lambda e: e.tensor_scalar(out=st[:, b0:b0 + TB], in0=st[:, b0:b0 + TB], scalar1=fv[:, scol:scol + 1], scalar2=None, op0=ALU.mult),
                     reads=[st_t, fv_t], writes=[st_t])
            tabs[nm] = (ct, ct_t, st, st_t)

    xs = Slots(P, 2, [128, DC, TB], F32, "xs")
    ws = Slots(P, 4, [128, DC, 128], BF16, "ws")
    if NT:
        wts = Slots(P, 2, [128, DC, 512], BF16, "wts")
        vst = Slots(P, 3, [128, 512], BF16, "vst")
    hT = P.sb([128, DC, TB], BF16, "hT"); hT_t = Trk()
    rstd = P.sb([128, TB], F32, "rstd"); rstd_t = Trk()
    rs2 = Ring(P, 2, [128, TB], F32, "rs2")
    sqr = Ring(P, 2, [128, TB], BF16, "sqr")
    ost = Slots(P, 3, [128, TB], out_dt, "ost")
    pr = PsumRing(P, 5)
    pstat = PsumRing(P, 2, "pst")

    for blk in range(NB):
        t0 = blk * TB
        xb, xb_t, xsem = xs.next()
        P.dma("sp", xb[:], xTv[:, :, t0:t0 + TB], xsem, writes=[xb_t])
        P.op("act", lambda e: e.activation(out=hT[:], in_=xb[:], func=AF.Square), reads=[xb_t], writes=[hT_t])
        pst, pst_t = pstat.next()
        for k in range(DC):
            P.op("pe", lambda e: e.matmul(pst[:], ones[:], hT[:, k, :], start=(k == 0), stop=(k == DC - 1)),
                 reads=[ones_t, hT_t], writes=[pst_t])
        P.op("dve", lambda e: e.tensor_scalar(out=rstd[:], in0=pst[:], scalar1=1.0 / D, scalar2=EPS, op0=ALU.mult, op1=ALU.add),
             reads=[pst_t], writes=[rstd_t])
        P.op("act", lambda e: e.activation(out=rstd[:], in_=rstd[:], func=AF.Sqrt), reads=[rstd_t], writes=[rstd_t])
        P.op("dve", lambda e: e.reciprocal(out=rstd[:], in_=rstd[:]), reads=[rstd_t], writes=[rstd_t])
        for c in range(DC):
            tt, tt_t = tmpr.next()
            P.op("dve", lambda e: e.tensor_tensor(out=tt[:], in0=xb[:, c, :], in1=rstd[:], op=ALU.mult),
                 reads=[xb_t, rstd_t], writes=[tt_t])
            P.op("act", lambda e: e.activation(out=hT[:, c, :], in_=tt[:], func=AF.Identity, scale=gm[:, c:c + 1], bias=vt[:, 2, c:c + 1]),
                 reads=[tt_t, gm_t, vt_t], writes=[hT_t])

        def mm_chunk(widx):
            wt, wt_t, wsem = ws.next()
            P.dma("sp", wt[:], wf[widx], wsem, writes=[wt_t])
            pt, pt_t = pr.next()
            for k in range(DC):
                P.op("pe", lambda e: e.matmul(pt[:], wt[:, k, :], hT[:, k, :], start=(k == 0), stop=(k == DC - 1)),
                     reads=[wt_t, hT_t], writes=[pt_t])
            return pt, pt_t

        for it in items:
            mode = it["mode"]
            pt, pt_t = mm_chunk(it["w"])
            ot, ot_t, osem = ost.next()
            if mode == "plain":
                P.op("act", lambda e: e.activation(out=ot[:], in_=pt[:], func=AF.Copy, scale=float(it.get("scale", 1.0))),
                     reads=[pt_t], writes=[ot_t])
            else:
                if mode in ("hn", "hn_rope"):
                    sq, sq_t = sqr.next()
                    P.op("act", lambda e: e.activation(out=sq[:], in_=pt[:], func=AF.Square), reads=[pt_t], writes=[sq_t])
                    p2, p2_t = pstat.next()
                    P.op("pe", lambda e: e.matmul(p2[:], ones[:], sq[:], start=True, stop=True), reads=[ones_t, sq_t], writes=[p2_t])
                    r2, r2_t = rs2.next()
                    P.op("dve", lambda e: e.tensor_scalar(out=r2[:], in0=p2[:], scalar1=1.0 / HD, scalar2=EPS, op0=ALU.mult, op1=ALU.add),
                         reads=[p2_t], writes=[r2_t])
                    P.op("act", lambda e: e.activation(out=r2[:], in_=r2[:], func=AF.Sqrt), reads=[r2_t], writes=[r2_t])
                    P.op("dve", lambda e: e.reciprocal(out=r2[:], in_=r2[:]), reads=[r2_t], writes=[r2_t])
                if mode == "hn":
                    g = it["g"]
                    P.op("dve", lambda e: e.scalar_tensor_tensor(out=ot[:], in0=pt[:], scalar=gv[:, g:g + 1], in1=r2[:], op0=ALU.mult, op1=ALU.mult),
                         reads=[pt_t, gv_t, r2_t], writes=[ot_t])
                else:
                    pts, pts_t = mm_chunk(it["ws"])
                    ct, ct_t, st, st_t = tabs["m" if mode == "hn_rope" else "i"]
                    u1, u1_t = tmpr.next()
                    u2, u2_t = tmpr.next()
                    if mode == "hn_rope":
                        g, gs = it["g"], it["gs"]
                        P.op("dve", lambda e: e.scalar_tensor_tensor(out=u1[:], in0=pt[:], scalar=gv[:, g:g + 1], in1=ct[:, t0:t0 + TB], op0=ALU.mult, op1=ALU.mult),
                             reads=[pt_t, gv_t, ct_t], writes=[u1_t])
                        P.op("dve", lambda e: e.scalar_tensor_tensor(out=u2[:], in0=pts[:], scalar=gv[:, gs:gs + 1], in1=st[:, t0:t0 + TB], op0=ALU.mult, op1=ALU.mult),
                             reads=[pts_t, gv_t, st_t], writes=[u2_t])
                        P.op("pool", lambda e: e.tensor_tensor(out=u1[:], in0=u1[:], in1=u2[:], op=ALU.add), reads=[u1_t, u2_t], writes=[u1_t])
                        P.op("dve", lambda e: e.tensor_tensor(out=ot[:], in0=u1[:], in1=r2[:], op=ALU.mult), reads=[u1_t, r2_t], writes=[ot_t])
                    else:
                        P.op("dve", lambda e: e.tensor_tensor(out=u1[:], in0=pt[:], in1=ct[:, t0:t0 + TB], op=ALU.mult), reads=[pt_t, ct_t], writes=[u1_t])
                        P.op("dve", lambda e: e.tensor_tensor(out=u2[:], in0=pts[:], in1=st[:, t0:t0 + TB], op=ALU.mult), reads=[pts_t, st_t], writes=[u2_t])
                        P.op("pool", lambda e: e.tensor_tensor(out=ot[:], in0=u1[:], in1=u2[:], op=ALU.add), reads=[u1_t, u2_t], writes=[ot_t])
            o = it["out"]
            P.dma("sp", yT[o * 128:(o + 1) * 128, t0:t0 + TB], ot[:], osem, reads=[ot_t])
        for g in range(NT):
            wt, wt_t, wsem = wts.next()
            P.dma("sp", wt[:], wt_d[g], wsem, writes=[wt_t])
            for tsb in range(TB // 128):
                pt, pt_t = pr.next()
                for k in range(DC):
                    P.op("pe", lambda e: e.matmul(pt[:], hT[:, k, tsb * 128:(tsb + 1) * 128], wt[:, k, :], start=(k == 0), stop=(k == DC - 1)),
                         reads=[wt_t, hT_t], writes=[pt_t])
                vt_, vt_t_, vsem = vst.next()
                P.op("act", lambda e: e.activation(out=vt_[:], in_=pt[:], func=AF.Copy), reads=[pt_t], writes=[vt_t_])
                P.dma("sp", vo[t0 + tsb * 128:t0 + (tsb + 1) * 128, g * 512:(g + 1) * 512], vt_[:], vsem, reads=[vt_t_])
    fin = list(ost.trk) + (list(vst.trk) if NT else [])
    return P.finish(fin)


def lay_wt(w):
    K, N = w.shape
    return np.ascontiguousarray(w.reshape(K // 128, 128, N // 512, 512).transpose(2, 1, 0, 3))


def _swap_cols(w, width):
    K, N = w.shape
    h = width // 2
    w4 = w.reshape(K, N // width, 2, h)
    return np.ascontiguousarray(w4[:, :, ::-1, :]).reshape(K, N)


def _pad_cols(w, n):
    K, N = w.shape
    if N == n:
        return w
    out = np.zeros((K, n), w.dtype)
    out[:, :N] = w
    return out


def rope_consts():
    p = np.arange(128)
    invf_m = (10000.0 ** (-(2.0 * (p % 64)) / 128.0)).astype(np.float32)
    sgn_m = np.where(p < 64, -1.0, 1.0).astype(np.float32)
    invf_i = (10000.0 ** (-(2.0 * (p % 32)) / 64.0)).astype(np.float32)
    sgn_i = np.where((p % 64) < 32, -1.0, 1.0).astype(np.float32)
    return np.stack([invf_m, sgn_m, invf_i, sgn_i], axis=1)


def proj_plan_sb(w_in, qg, kg):
    chunks, items = [], []
    for h in range(2 * NH):
        chunks.append(w_in[:, h * 128:(h + 1) * 128])
        items.append(dict(mode="hn", w=len(chunks) - 1, g=0 if h < NH else 1, out=h))
    wf = np.stack([lay_w(c)[0] for c in chunks])
    wt = lay_wt(w_in[:, 2 * D:3 * D])
    gv = np.stack([qg, kg], axis=1).astype(np.float32)
    return wf, items, 2 * NH, wt, gv


def proj_plan_s5(w_in):
    chunks, items = [], []
    for c in range(DC):
        chunks.append(w_in[:, c * 128:(c + 1) * 128])
        items.append(dict(mode="plain", w=c, out=c))
    wf = np.stack([lay_w(c)[0] for c in chunks])
    return wf, items, DC, None, np.zeros((128, 1), np.float32)


def proj_plan_dsa(w_in, qg, kg):
    chunks, items = [], []

    def add(c):
        chunks.append(np.ascontiguousarray(c))
        return len(chunks) - 1

    for h in range(2 * NH):
        c = w_in[:, h * 128:(h + 1) * 128]
        isq = h < NH
        items.append(dict(mode="hn_rope", w=add(c), ws=add(_swap_cols(c, 128)), g=0 if isq else 2, gs=1 if isq else 3, out=h))
    o = 2 * NH
    qi0 = 3 * D
    for c8 in range(8):
        c = w_in[:, qi0 + c8 * 128: qi0 + (c8 + 1) * 128]
        items.append(dict(mode="rope_i", w=add(c), ws=add(_swap_cols(c, 64)), out=o + c8))
    ki = w_in[:, qi0 + 1024: qi0 + 1088]
    ki2 = np.concatenate([ki, ki], axis=1)
    items.append(dict(mode="rope_i", w=add(ki2), ws=add(_swap_cols(ki2, 64)), out=o + 8))
    wi = _pad_cols(w_in[:, qi0 + 1088: qi0 + 1104], 128)
    items.append(dict(mode="plain", w=add(wi), out=o + 9, scale=16 ** -0.5))
    wf = np.stack([lay_w(c)[0] for c in chunks])
    wt = lay_wt(w_in[:, 2 * D:3 * D])
    sw = lambda g: np.concatenate([g[64:], g[:64]])
    gv = np.stack([qg, sw(qg), kg, sw(kg)], axis=1).astype(np.float32)
    return wf, items, o + 10, wt, gv


def build_sb(Sq, NHC):
    P = Prog()
    KB = Sq // 128
    QB = Sq // 512
    scale = HD ** -0.5
    qT = P.dram("qT", [NHC, 128, Sq], BF16, "ExternalInput")
    kT = P.dram("kT", [NHC, 128, Sq], BF16, "ExternalInput")
    vv = P.dram("v", [NHC, 128, KB, 128], BF16, "ExternalInput")
    oT = P.dram("oT", [NHC * 128, Sq], BF16, "ExternalOutput")

    ones = P.sb([128, 128], BF16, "ones"); ones_t = Trk()
    tri = P.sb([128, 128], BF16, "tri"); tri_t = Trk()
    P.op("dve", lambda e: e.memset(ones[:], 1.0), writes=[ones_t])
    P.op("pool", lambda e: e.affine_select(out=tri[:], in_=ones[:], pattern=[[-1, 128]], compare_op=ALU.is_gt, fill=0.0, base=0, channel_multiplier=1),
         reads=[ones_t], writes=[tri_t])
    onesW = P.sb([128, 512], BF16, "onesW"); onesW_t = Trk()
    P.op("dve", lambda e: e.memset(onesW[:], 1.0), writes=[onesW_t])
    dmask = []
    dmask_t = Trk()
    for d_ in range(4):
        m_ = P.sb([128, 512], BF16, f"dmask{d_}")
        P.op("pool", lambda e: e.affine_select(out=m_[:], in_=onesW[:], pattern=[[1, 512]], compare_op=ALU.is_gt, fill=0.0,
                                               base=-128 * d_, channel_multiplier=-1), reads=[onesW_t], writes=[dmask_t])
        dmask.append(m_)
    qs = Slots(P, 2, [128, Sq], BF16, "qs")
    ks = Slots(P, 2, [128, Sq], BF16, "ks")
    vs = Slots(P, 2, [128, KB, 128], BF16, "vs")
    er = Ring(P, 2, [128, 512], F32, "e")
    spr = Ring(P, 3, [128, 512], F32, "sp")
    spmr = Ring(P, 3, [128, 512], BF16, "spm")
    cumr = Ring(P, 2, [128, 512], F32, "cum")
    a1r = Ring(P, 3, [128, 512], F32, "a1")
    wr = Ring(P, 3, [128, 512], BF16, "w")
    R = P.sb([128, 512], F32, "R"); R_t = Trk()
    ost = Slots(P, 2, [128, 512], BF16, "ost")
    pz = PsumRing(P, 2, "pz")
    ptri = PsumRing(P, 2, "ptri")
    pone = PsumRing(P, 2, "pone")
    pacc = PsumRing(P, 2, "pacc")

    pairs = []
    for h in range(NHC):
        for qb in range(QB):
            kmax = 4 * qb + 3
            for kb in range(kmax, -1, -1):
                pairs.append(dict(h=h, qb=qb, kb=kb, kmax=kmax, first=(kb == kmax), last=(kb == 0), diag=(kb >= 4 * qb)))
    heads = {}
    accs = {}

    def stage_a(st):
        h, qb, kb = st["h"], st["qb"], st["kb"]
        if h not in heads:
            qt, qt_t, qsem = qs.next()
            kt, kt_t, ksem = ks.next()
            vt, vt_t, vsem = vs.next()
            P.dma("sp", qt[:], qT[h], qsem, writes=[qt_t])
            P.dma("sp", kt[:], kT[h], ksem, writes=[kt_t])
            P.dma("sp", vt[:], vv[h], vsem, writes=[vt_t])
            heads[h] = (qt, qt_t, kt, kt_t, vt, vt_t)
        qt, qt_t, kt, kt_t, vt, vt_t = heads[h]
        q0 = qb * 512
        z, z_t = pz.next()
        P.op("pe", lambda e: e.matmul(z[:], kt[:, kb * 128:(kb + 1) * 128], qt[:, q0:q0 + 512], start=True, stop=True),
             reads=[kt_t, qt_t], writes=[z_t])
        ee, ee_t = er.next()
        P.op("act", lambda e: e.activation(out=ee[:], in_=z[:], func=AF.Exp, scale=scale), reads=[z_t], writes=[ee_t])
        sp, sp_t = spr.next()
        P.op("act", lambda e: e.activation(out=sp[:], in_=ee[:], func=AF.Ln, bias=1.0, scale=1.0), reads=[ee_t], writes=[sp_t])
        spm, spm_t = spmr.next()
        if st["diag"]:
            dm = dmask[kb - 4 * qb]
            P.op("pool", lambda e: e.tensor_tensor(out=spm[:], in0=sp[:], in1=dm[:], op=ALU.mult), reads=[sp_t, dmask_t], writes=[spm_t])
        else:
            P.op("pool", lambda e: e.tensor_copy(out=spm[:], in_=sp[:]), reads=[sp_t], writes=[spm_t])
        pt, pt_t = ptri.next()
        P.op("pe", lambda e: e.matmul(pt[:], tri[:], spm[:], start=True, stop=True), reads=[tri_t, spm_t], writes=[pt_t])
        po, po_t = pone.next()
        P.op("pe", lambda e: e.matmul(po[:], ones[:], spm[:], start=True, stop=True), reads=[ones_t, spm_t], writes=[po_t])
        st.update(z=z, z_t=z_t, sp=sp, sp_t=sp_t, pt=pt, pt_t=pt_t, po=po, po_t=po_t)

    def stage_b(st):
        h, qb, kb, kmax = st["h"], st["qb"], st["kb"], st["kmax"]
        qt, qt_t, kt, kt_t, vt, vt_t = heads[h]
        z, z_t, sp, sp_t, pt, pt_t, po, po_t = [st[k] for k in ("z", "z_t", "sp", "sp_t", "pt", "pt_t", "po", "po_t")]
        if st["first"]:
            P.op("pool", lambda e: e.memset(R[:], 0.0), writes=[R_t])
            accs[(h, qb)] = pacc.next()
        acc, acc_t = accs[(h, qb)]
        cum, cum_t = cumr.next()
        P.op("dve", lambda e: e.tensor_tensor(out=cum[:], in0=pt[:], in1=R[:], op=ALU.add), reads=[pt_t, R_t], writes=[cum_t])
        a1, a1_t = a1r.next()
        P.op("dve", lambda e: e.scalar_tensor_tensor(out=a1[:], in0=z[:], scalar=scale, in1=sp[:], op0=ALU.mult, op1=ALU.subtract),
             reads=[z_t, sp_t], writes=[a1_t])
        P.op("pool", lambda e: e.tensor_tensor(out=a1[:], in0=a1[:], in1=cum[:], op=ALU.subtract), reads=[a1_t, cum_t], writes=[a1_t])
        w, w_t = wr.next()
        P.op("act", lambda e: e.activation(out=w[:], in_=a1[:], func=AF.Exp), reads=[a1_t], writes=[w_t])
        if st["diag"]:
            dm = dmask[kb - 4 * qb]
            P.op("pool", lambda e: e.tensor_tensor(out=w[:], in0=w[:], in1=dm[:], op=ALU.mult), reads=[w_t, dmask_t], writes=[w_t])
        P.op("pe", lambda e: e.matmul(acc[:], vt[:, kb, :], w[:], start=(kb == kmax), stop=(kb == 0)), reads=[vt_t, w_t], writes=[acc_t])
        if kb > 0:
            P.op("dve", lambda e: e.tensor_tensor(out=R[:], in0=po[:], in1=R[:], op=ALU.add), reads=[po_t, R_t], writes=[R_t])
        if st["last"]:
            q0 = qb * 512
            ot, ot_t, osem = ost.next()
            P.op("act", lambda e: e.activation(out=ot[:], in_=acc[:], func=AF.Copy), reads=[acc_t], writes=[ot_t])
            P.dma("sp", oT[h * 128:(h + 1) * 128, q0:q0 + 512], ot[:], osem, reads=[ot_t])

    for i in range(len(pairs) + 1):
        if i < len(pairs):
            stage_a(pairs[i])
        if i > 0:
            stage_b(pairs[i - 1])
    return P.finish(ost.trk)


def build_dsa(Sq, NHC, TOPK, NIT=24):
    P = Prog()
    KB = Sq // 128
    QG = Sq // 512
    scale = HD ** -0.5
    qT = P.dram("qT", [NHC, 128, Sq], BF16, "ExternalInput")
    kT = P.dram("kT", [NHC, 128, Sq], BF16, "ExternalInput")
    vv = P.dram("v", [NHC, 128, KB, 128], BF16, "ExternalInput")
    qiT = P.dram("qiT", [8, 128, Sq], BF16, "ExternalInput")
    kiT = P.dram("kiT", [128, Sq], BF16, "ExternalInput")
    wi_d = P.dram("wi", [128, KB, 16], BF16, "ExternalInput")
    idn_d = P.dram("idn", [128, 128], BF16, "ExternalInput")
    oT = P.dram("oT", [NHC * 128, Sq], BF16, "ExternalOutput")

    s0 = P.dsem()
    ones = P.sb([128, 128], BF16, "ones"); ones_t = Trk()
    P.op("dve", lambda e: e.memset(ones[:], 1.0), writes=[ones_t])
    idn = P.sb([128, 128], BF16, "idns"); idn_t = Trk()
    P.dma("sp", idn[:], idn_d, s0, writes=[idn_t])
    qi = P.sb([128, 8, Sq], BF16, "qis"); qi_t = Trk()
    P.dma("sp", qi[:], qiT.rearrange("c p s -> p c s"), s0, writes=[qi_t])
    ki = P.sb([128, Sq], BF16, "kis"); ki_t = Trk()
    P.dma("sp", ki[:], kiT, s0, writes=[ki_t])
    wib = P.sb([128, KB, 16], BF16, "wib"); wib_t = Trk()
    P.dma("sp", wib[:], wi_d, s0, writes=[wib_t])
    wi = P.sb([128, KB, 16], F32, "wif"); wi_t = Trk()
    P.op("dve", lambda e: e.tensor_copy(out=wi[:], in_=wib[:]), reads=[wib_t], writes=[wi_t])

    cmask = P.sb([128, 128], F32, "cmask"); cmask_t = Trk()
    P.op("dve", lambda e: e.memset(cmask[:], 3e38), writes=[cmask_t])
    P.op("pool", lambda e: e.affine_select(out=cmask[:], in_=cmask[:], pattern=[[-1, 128]], compare_op=ALU.is_ge, fill=-3e38, base=0, channel_multiplier=1),
         reads=[cmask_t], writes=[cmask_t])
    accr = Ring(P, 2, [128, Sq], F32, "acc")
    mq = P.sb([128, Sq], BF16, "mq"); mq_t = Trk()
    maskT = P.sb([128, KB, 512], BF16, "maskT"); maskT_t = Trk()
    relr = Ring(P, 3, [128, 512], F32, "rel")
    sm = {n: (P.sb([128, 1], F32, "sm_" + n), Trk()) for n in ("mx", "mn", "w0", "lo", "hk", "mid", "cnt", "tmp")}
    ks = Slots(P, 2, [128, Sq], BF16, "ks")
    vs = Slots(P, 2, [128, KB, 128], BF16, "vs")
    qs = Slots(P, 2, [128, 512], BF16, "qs")
    pr_ = Ring(P, 3, [128, 512], BF16, "p")
    pmr = Ring(P, 3, [128, 512], BF16, "pm")
    rd = P.sb([128, 512], F32, "rd"); rd_t = Trk()
    ost = Slots(P, 2, [128, 512], BF16, "ost")
    prel = PsumRing(P, 2, "prel")
    ptr = [P.ps([128, 512], BF16, name=f"ptr{i}") for i in range(2)]
    ptr_t = [Trk(), Trk()]
    pz = PsumRing(P, 2, "pz")
    pacc = P.ps(name="pacc"); pacc_t = Trk()
    pden = P.ps(name="pden"); pden_t = Trk()
    tri_i = [0]

    for qg in range(QG):
        q0g = qg * 512
        kmax = 4 * qg + 3
        for qsub in range(4):
            qt = 4 * qg + qsub
            nk = (qt + 1) * 128
            acc, acc_t = accr.next()
            for g in range((nk + 511) // 512):
                kw = min(512, nk - g * 512)
                for ih in range(16):
                    c, half = ih // 2, ih % 2
                    rows = slice(64 * half, 64 * half + 64)
                    pp, pp_t = prel.next()
                    P.op("pe", lambda e: e.matmul(pp[:, :kw], qi[rows, c, qt * 128:(qt + 1) * 128], ki[rows, g * 512:g * 512 + kw], start=True, stop=True),
                         reads=[qi_t, ki_t], writes=[pp_t])
                    rl, rl_t = relr.next()
                    P.op("act", lambda e: e.activation(out=rl[:, :kw], in_=pp[:, :kw], func=AF.Relu, scale=0.125), reads=[pp_t], writes=[rl_t])
                    dst = acc[:, g * 512:g * 512 + kw]
                    if ih == 0:
                        P.op("dve", lambda e: e.tensor_scalar(out=dst, in0=rl[:, :kw], scalar1=wi[:, qt, 0:1], scalar2=None, op0=ALU.mult),
                             reads=[rl_t, wi_t], writes=[acc_t])
                    else:
                        P.op("dve", lambda e: e.scalar_tensor_tensor(out=dst, in0=rl[:, :kw], scalar=wi[:, qt, ih:ih + 1], in1=dst, op0=ALU.mult, op1=ALU.add),
                             reads=[rl_t, wi_t], writes=[acc_t])
            (mx, mx_t), (mn, mn_t), (w0, w0_t), (lo, lo_t) = sm["mx"], sm["mn"], sm["w0"], sm["lo"]
            (hk, hk_t), (mid, mid_t), (cnt, cnt_t), (tmp, tmp_t) = sm["hk"], sm["mid"], sm["cnt"], sm["tmp"]
            if nk > TOPK:
                P.op("dve", lambda e: e.tensor_reduce(out=mx[:], in_=acc[:, :nk], axis=AX.X, op=ALU.max), reads=[acc_t], writes=[mx_t])
                P.op("dve", lambda e: e.tensor_reduce(out=lo[:], in_=acc[:, :nk], axis=AX.X, op=ALU.min), reads=[acc_t], writes=[lo_t])
                P.op("dve", lambda e: e.scalar_tensor_tensor(out=w0[:], in0=mx[:], scalar=1.0, in1=lo[:], op0=ALU.add, op1=ALU.subtract),
                     reads=[mx_t, lo_t], writes=[w0_t])
            else:
                P.op("dve", lambda e: e.memset(lo[:], -1e30), writes=[lo_t])
            dg = acc[:, qt * 128:(qt + 1) * 128]
            P.op("dve", lambda e: e.tensor_tensor(out=dg, in0=dg, in1=cmask[:], op=ALU.min), reads=[acc_t, cmask_t], writes=[acc_t])
            if nk > TOPK:
                for it in range(NIT):
                    P.op("dve", lambda e: e.tensor_scalar(out=hk[:], in0=w0[:], scalar1=float(2.0 ** -(it + 1)), scalar2=None, op0=ALU.mult),
                         reads=[w0_t], writes=[hk_t])
                    P.op("dve", lambda e: e.tensor_tensor(out=mid[:], in0=lo[:], in1=hk[:], op=ALU.add), reads=[lo_t, hk_t], writes=[mid_t])
                    P.op("pool", lambda e: e.memset(cnt[:], 0.0), writes=[cnt_t])
                    P.op("dve", lambda e: e.tensor_scalar(out=mq[:, :nk], in0=acc[:, :nk], scalar1=mid[:, 0:1], scalar2=0.0, op0=ALU.is_ge, op1=ALU.add,
                                                          accum_out=cnt[:]), reads=[acc_t, mid_t, cnt_t], writes=[mq_t, cnt_t])
                    P.op("dve", lambda e: e.scalar_tensor_tensor(out=tmp[:], in0=cnt[:], scalar=TOPK - 0.5, in1=hk[:], op0=ALU.is_ge, op1=ALU.mult),
                         reads=[cnt_t, hk_t], writes=[tmp_t])
                    P.op("dve", lambda e: e.tensor_tensor(out=lo[:], in0=lo[:], in1=tmp[:], op=ALU.add), reads=[lo_t, tmp_t], writes=[lo_t])
            P.op("dve", lambda e: e.tensor_scalar(out=mq[:, :nk], in0=acc[:, :nk], scalar1=lo[:, 0:1], scalar2=None, op0=ALU.is_ge),
                 reads=[acc_t, lo_t], writes=[mq_t])
            for kb0 in range(0, qt + 1, 4):
                nb = min(4, qt + 1 - kb0)
                j = tri_i[0] % 2
                tri_i[0] += 1
                for b in range(nb):
                    kb = kb0 + b
                    P.op("pe", lambda e: e.transpose(ptr[j][:, b * 128:(b + 1) * 128], mq[:, kb * 128:(kb + 1) * 128], idn[:]),
                         reads=[mq_t, idn_t], writes=[ptr_t[j]])
                src = ptr[j][:, :nb * 128].rearrange("p (b q) -> p b q", q=128)
                P.op("act", lambda e: e.activation(out=maskT[:, kb0:kb0 + nb, qsub * 128:(qsub + 1) * 128], in_=src, func=AF.Copy),
                     reads=[ptr_t[j]], writes=[maskT_t])
            if qt < kmax:
                P.op("pool", lambda e: e.memset(maskT[:, qt + 1:kmax + 1, qsub * 128:(qsub + 1) * 128], 0.0), writes=[maskT_t])
        nkg = (kmax + 1) * 128
        for h in range(NHC):
            kt, kt_t, ksem = ks.next()
            vt, vt_t, vsem = vs.next()
            qt_, qt_t, qsem = qs.next()
            P.dma("sp", kt[:, :nkg], kT[h, :, :nkg], ksem, writes=[kt_t])
            P.dma("sp", vt[:, :kmax + 1, :], vv[h, :, :kmax + 1, :], vsem, writes=[vt_t])
            P.dma("sp", qt_[:], qT[h, :, q0g:q0g + 512], qsem, writes=[qt_t])
            for kb in range(kmax + 1):
                z, z_t = pz.next()
                P.op("pe", lambda e: e.matmul(z[:], kt[:, kb * 128:(kb + 1) * 128], qt_[:], start=True, stop=True), reads=[kt_t, qt_t], writes=[z_t])
                p_, p_t = pr_.next()
                P.op("act", lambda e: e.activation(out=p_[:], in_=z[:], func=AF.Exp, scale=scale), reads=[z_t], writes=[p_t])
                pm, pm_t = pmr.next()
                P.op("pool", lambda e: e.tensor_tensor(out=pm[:], in0=p_[:], in1=maskT[:, kb, :], op=ALU.mult), reads=[p_t, maskT_t], writes=[pm_t])
                P.op("pe", lambda e: e.matmul(pacc[:], vt[:, kb, :], pm[:], start=(kb == 0), stop=(kb == kmax)), reads=[vt_t, pm_t], writes=[pacc_t])
                P.op("pe", lambda e: e.matmul(pden[:], ones[:], pm[:], start=(kb == 0), stop=(kb == kmax)), reads=[ones_t, pm_t], writes=[pden_t])
            P.op("dve", lambda e: e.reciprocal(out=rd[:], in_=pden[:]), reads=[pden_t], writes=[rd_t])
            ot, ot_t, osem = ost.next()
            P.op("dve", lambda e: e.tensor_tensor(out=ot[:], in0=pacc[:], in1=rd[:], op=ALU.mult), reads=[pacc_t, rd_t], writes=[ot_t])
            P.dma("sp", oT[h * 128:(h + 1) * 128, q0g:q0g + 512], ot[:], osem, reads=[ot_t])
    return P.finish(ost.trk)


def build_s5(Sq):
    P = Prog()
    NP_, NCT = 32, 8
    NSC = Sq // 512
    uT = P.dram("uT", [NCT * 128, Sq], F32, "ExternalInput")
    prm_d = P.dram("prm", [128, 3, NP_], F32, "ExternalInput")
    Bt_d = P.dram("Bt", [2, NP_, 128, 128], F32, "ExternalInput")
    Ct_d = P.dram("Ct", [2, NP_, 128, 128], F32, "ExternalInput")
    dv_d = P.dram("dv", [128, NCT], F32, "ExternalInput")
    tl_d = P.dram("tl", [128, 128], F32, "ExternalInput")
    zT = P.dram("zT", [NCT * 128, Sq], BF16, "ExternalOutput")

    s0 = P.dsem()
    prm = P.sb([128, 3, NP_], F32, "prms"); prm_t = Trk()
    P.dma("sp", prm[:], prm_d, s0, writes=[prm_t])
    dv = P.sb([128, NCT], F32, "dvs"); dv_t = Trk()
    P.dma("sp", dv[:], dv_d, s0, writes=[dv_t])
    tl = P.sb([128, 128], F32, "tls"); tl_t = Trk()
    P.dma("sp", tl[:], tl_d, s0, writes=[tl_t])
    onesF = P.sb([128, 128], F32, "onesF"); onesF_t = Trk()
    P.op("dve", lambda e: e.memset(onesF[:], 1.0), writes=[onesF_t])

    def small(name):
        return P.sb([128, NP_], F32, name), Trk()

    lr, li, ldt = prm[:, 0, :], prm[:, 1, :], prm[:, 2, :]
    (dt, dt_t), (th, th_t), (rho, rho_t), (cs, cs_t), (sn, sn_t) = small("dt"), small("th"), small("rho"), small("cs"), small("sn")
    (ar, ar_t), (ai, ai_t), (den, den_t), (am1, am1_t) = small("ar"), small("ai"), small("den"), small("am1")
    (fr, fr_t), (fi, fi_t), (nfr, nfr_t), (t1s, t1s_t) = small("fr"), small("fi"), small("nfr"), small("t1s")
    rf = Ring(P, 3, [128, 128], F32, "rf")
    ri = Ring(P, 2, [128, 128], I32, "rii")
    P.op("act", lambda e: e.activation(out=dt[:], in_=ldt, func=AF.Exp), reads=[prm_t], writes=[dt_t])
    P.op("dve", lambda e: e.tensor_tensor(out=th[:], in0=li, in1=dt[:], op=ALU.mult), reads=[prm_t, dt_t], writes=[th_t])
    P.op("dve", lambda e: e.tensor_tensor(out=rho[:], in0=lr, in1=dt[:], op=ALU.mult), reads=[prm_t, dt_t], writes=[rho_t])
    P.op("act", lambda e: e.activation(out=rho[:], in_=rho[:], func=AF.Exp), reads=[rho_t], writes=[rho_t])
    emit_sin(P, cs[:], cs_t, th[:], th_t, [128, NP_], rf, ri, shift=float(np.pi / 2))
    emit_sin(P, sn[:], sn_t, th[:], th_t, [128, NP_], rf, ri)
    P.op("dve", lambda e: e.tensor_tensor(out=ar[:], in0=rho[:], in1=cs[:], op=ALU.mult), reads=[rho_t, cs_t], writes=[ar_t])
    P.op("dve", lambda e: e.tensor_tensor(out=ai[:], in0=rho[:], in1=sn[:], op=ALU.mult), reads=[rho_t, sn_t], writes=[ai_t])
    P.op("dve", lambda e: e.tensor_tensor(out=den[:], in0=lr, in1=lr, op=ALU.mult), reads=[prm_t], writes=[den_t])
    P.op("dve", lambda e: e.tensor_tensor(out=t1s[:], in0=li, in1=li, op=ALU.mult), reads=[prm_t], writes=[t1s_t])
    P.op("dve", lambda e: e.tensor_tensor(out=den[:], in0=den[:], in1=t1s[:], op=ALU.add), reads=[den_t, t1s_t], writes=[den_t])
    P.op("dve", lambda e: e.reciprocal(out=den[:], in_=den[:]), reads=[den_t], writes=[den_t])
    P.op("dve", lambda e: e.tensor_scalar(out=am1[:], in0=ar[:], scalar1=-1.0, scalar2=None, op0=ALU.add), reads=[ar_t], writes=[am1_t])
    P.op("dve", lambda e: e.tensor_tensor(out=fr[:], in0=am1[:], in1=lr, op=ALU.mult), reads=[am1_t, prm_t], writes=[fr_t])
    P.op("dve", lambda e: e.tensor_tensor(out=t1s[:], in0=ai[:], in1=li, op=ALU.mult), reads=[ai_t, prm_t], writes=[t1s_t])
    P.op("dve", lambda e: e.tensor_tensor(out=fr[:], in0=fr[:], in1=t1s[:], op=ALU.add), reads=[fr_t, t1s_t], writes=[fr_t])
    P.op("dve", lambda e: e.tensor_tensor(out=fr[:], in0=fr[:], in1=den[:], op=ALU.mult), reads=[fr_t, den_t], writes=[fr_t])
    P.op("dve", lambda e: e.tensor_tensor(out=fi[:], in0=ai[:], in1=lr, op=ALU.mult), reads=[ai_t, prm_t], writes=[fi_t])
    P.op("dve", lambda e: e.tensor_tensor(out=t1s[:], in0=am1[:], in1=li, op=ALU.mult), reads=[am1_t, prm_t], writes=[t1s_t])
    P.op("dve", lambda e: e.tensor_tensor(out=fi[:], in0=fi[:], in1=t1s[:], op=ALU.subtract), reads=[fi_t, t1s_t], writes=[fi_t])
    P.op("dve", lambda e: e.tensor_tensor(out=fi[:], in0=fi[:], in1=den[:], op=ALU.mult), reads=[fi_t, den_t], writes=[fi_t])
    P.op("dve", lambda e: e.tensor_scalar(out=nfr[:], in0=fr[:], scalar1=-1.0, scalar2=None, op0=ALU.mult), reads=[fr_t], writes=[nfr_t])
    ar3 = P.sb([128, NP_, 1], F32, "ar3"); ai3 = P.sb([128, NP_, 1], F32, "ai3"); a3_t = Trk()
    P.op("dve", lambda e: e.tensor_copy(out=ar3[:, :, 0], in_=ar[:]), reads=[ar_t], writes=[a3_t])
    P.op("dve", lambda e: e.tensor_copy(out=ai3[:, :, 0], in_=ai[:]), reads=[ai_t], writes=[a3_t])

    cosT = P.sb([128, NP_, 128], F32, "cosT"); sinT = P.sb([128, NP_, 128], F32, "sinT")
    GrT = P.sb([128, NP_, 128], F32, "GrT"); GiT = P.sb([128, NP_, 128], F32, "GiT")
    rhoT = P.sb([128, NP_, 128], F32, "rhoT")
    tab_t = Trk()
    ang = P.sb([128, 128], F32, "ang"); ang_t = Trk()
    for i in range(NP_):
        P.op("dve", lambda e: e.tensor_scalar(out=ang[:], in0=tl[:], scalar1=th[:, i:i + 1], scalar2=None, op0=ALU.mult), reads=[tl_t, th_t], writes=[ang_t])
        emit_sin(P, cosT[:, i, :], tab_t, ang[:], ang_t, [128, 128], rf, ri, shift=float(np.pi / 2))
        emit_sin(P, sinT[:, i, :], tab_t, ang[:], ang_t, [128, 128], rf, ri)
        P.op("pool", lambda e: e.tensor_scalar(out=GrT[:, i, :], in0=cosT[:, i, :], scalar1=fr[:, i:i + 1], scalar2=None, op0=ALU.mult), reads=[tab_t, fr_t], writes=[tab_t])
        P.op("dve", lambda e: e.scalar_tensor_tensor(out=GrT[:, i, :], in0=sinT[:, i, :], scalar=fi[:, i:i + 1], in1=GrT[:, i, :], op0=ALU.mult, op1=ALU.add), reads=[tab_t, fi_t], writes=[tab_t])
        P.op("pool", lambda e: e.tensor_scalar(out=GiT[:, i, :], in0=cosT[:, i, :], scalar1=fi[:, i:i + 1], scalar2=None, op0=ALU.mult), reads=[tab_t, fi_t], writes=[tab_t])
        P.op("dve", lambda e: e.scalar_tensor_tensor(out=GiT[:, i, :], in0=sinT[:, i, :], scalar=nfr[:, i:i + 1], in1=GiT[:, i, :], op0=ALU.mult, op1=ALU.add), reads=[tab_t, nfr_t], writes=[tab_t])
        P.op("pool", lambda e: e.tensor_scalar(out=rhoT[:, i, :], in0=onesF[:], scalar1=rho[:, i:i + 1], scalar2=None, op0=ALU.mult), reads=[onesF_t, rho_t], writes=[tab_t])
    P.op("pool", lambda e: e.memset(rhoT[:, :, 0:1], 0.0), writes=[tab_t])

    Bb = P.sb([128, 2, NP_, 128], BF16, "Bb"); Cb = P.sb([128, 2, NP_, 128], BF16, "Cb"); bc_t = Trk()
    stg = Slots(P, 1, [128, NP_, 128], F32, "stg")
    for (dst, src) in ((Bb, Bt_d), (Cb, Ct_d)):
        for ri_ in range(2):
            st, st_t, ssem = stg.next()
            P.dma("sp", st[:], src[ri_].rearrange("i k m -> k i m"), ssem, writes=[st_t])
            P.op("act", lambda e: e.activation(out=dst[:, ri_, :, :], in_=st[:], func=AF.Copy), reads=[st_t], writes=[bc_t])

    hpr = P.sb([128, NP_, 1], F32, "hpr"); hpi = P.sb([128, NP_, 1], F32, "hpi")
    hp_t = [Trk() for _ in range(NCT)]
    P.op("dve", lambda e: e.memset(hpr[:], 0.0), writes=hp_t)
    P.op("dve", lambda e: e.memset(hpi[:], 0.0), writes=hp_t)
    us = [Slots(P, 1, [128, 512], F32, f"us{c}_") for c in range(NCT)]
    ub = [Ring(P, 1, [128, 512], BF16, f"ub{c}_") for c in range(NCT)]
    wk = {n: Ring(P, 2, [128, 4, 128], F32, "wk_" + n) for n in ("t1", "t2", "br", "bi", "hr", "hi")}
    hb = {n: Ring(P, 2, [128, 4, 128], BF16, "hb_" + n) for n in ("r", "i")}
    c4 = {n: Ring(P, 2, [128, 4, 1], F32, "c4_" + n) for n in ("a", "b", "c", "d")}
    ep = {n: Ring(P, 2, [128, 512], F32, "ep_" + n) for n in ("y", "w")}
    ost = Slots(P, 2, [128, 512], BF16, "ost")
    pP = [P.ps([128, 4, 128], F32, name=f"pP{i}") for i in range(4)]
    pP_t = [Trk() for _ in range(4)]
    py = [P.ps([128, 512], F32, name=f"py{i}") for i in range(4)]
    py_t = [Trk() for _ in range(4)]
    pi_ = [0]

    for sc in range(NSC):
        cur = []
        for ct in range(NCT):
            ut, ut_t, usem = us[ct].next()
            P.dma("sp", ut[:], uT[ct * 128:(ct + 1) * 128, sc * 512:(sc + 1) * 512], usem, writes=[ut_t])
            ubt, ubt_t = ub[ct].next()
            P.op("act", lambda e: e.activation(out=ubt[:], in_=ut[:], func=AF.Copy), reads=[ut_t], writes=[ubt_t])
            cur.append((ut, ut_t, ubt, ubt_t))
        for cth in range(2):
            for n in range(4):
                for cl in range(4):
                    ct = cth * 4 + cl
                    ut, ut_t, ubt, ubt_t = cur[ct]
                    sl4 = slice(ct * 4, ct * 4 + 4)
                    j = pi_[0] % 2
                    pi_[0] += 1
                    Pr, Pr_t, Pi, Pi_t = pP[2 * j], pP_t[2 * j], pP[2 * j + 1], pP_t[2 * j + 1]
                    for pit in range(4):
                        i = ct * 4 + pit
                        P.op("pe", lambda e: e.matmul(Pr[:, pit, :], Bb[:, 0, i, :], ubt[:, n * 128:(n + 1) * 128], start=True, stop=True),
                             reads=[bc_t, ubt_t], writes=[Pr_t])
                        P.op("pe", lambda e: e.matmul(Pi[:, pit, :], Bb[:, 1, i, :], ubt[:, n * 128:(n + 1) * 128], start=True, stop=True),
                             reads=[bc_t, ubt_t], writes=[Pi_t])
                    (t1, t1_t), (t2, t2_t) = wk["t1"].next(), wk["t2"].next()
                    (br, br_t), (bi, bi_t) = wk["br"].next(), wk["bi"].next()
                    G4r, G4i = GrT[:, sl4, :], GiT[:, sl4, :]
                    P.op("dve", lambda e: e.tensor_tensor(out=t1[:], in0=Pr[:], in1=G4r, op=ALU.mult), reads=[Pr_t, tab_t], writes=[t1_t])
                    P.op("dve", lambda e: e.tensor_tensor(out=t2[:], in0=Pi[:], in1=G4i, op=ALU.mult), reads=[Pi_t, tab_t], writes=[t2_t])
                    P.op("pool", lambda e: e.tensor_tensor(out=br[:], in0=t1[:], in1=t2[:], op=ALU.subtract), reads=[t1_t, t2_t], writes=[br_t])
                    (t3, t3_t), (t4, t4_t) = wk["t1"].next(), wk["t2"].next()
                    P.op("dve", lambda e: e.tensor_tensor(out=t3[:], in0=Pi[:], in1=G4r, op=ALU.mult), reads=[Pi_t, tab_t], writes=[t3_t])
                    P.op("dve", lambda e: e.tensor_tensor(out=t4[:], in0=Pr[:], in1=G4i, op=ALU.mult), reads=[Pr_t, tab_t], writes=[t4_t])
                    P.op("pool", lambda e: e.tensor_tensor(out=bi[:], in0=t3[:], in1=t4[:], op=ALU.add), reads=[t3_t, t4_t], writes=[bi_t])
                    (ca, ca_t), (cb, cb_t), (cc, cc_t), (cd, cd_t) = c4["a"].next(), c4["b"].next(), c4["c"].next(), c4["d"].next()
                    a4r, a4i, h4r, h4i = ar3[:, sl4, :], ai3[:, sl4, :], hpr[:, sl4, :], hpi[:, sl4, :]
                    hpt = hp_t[ct]
                    P.op("dve", lambda e: e.tensor_tensor(out=ca[:], in0=a4r, in1=h4r, op=ALU.mult), reads=[a3_t, hpt], writes=[ca_t])
                    P.op("dve", lambda e: e.tensor_tensor(out=cb[:], in0=a4i, in1=h4i, op=ALU.mult), reads=[a3_t, hpt], writes=[cb_t])
                    P.op("dve", lambda e: e.tensor_tensor(out=ca[:], in0=ca[:], in1=cb[:], op=ALU.subtract), reads=[ca_t, cb_t], writes=[ca_t])
                    P.op("dve", lambda e: e.tensor_tensor(out=cc[:], in0=a4r, in1=h4i, op=ALU.mult), reads=[a3_t, hpt], writes=[cc_t])
                    P.op("dve", lambda e: e.tensor_tensor(out=cd[:], in0=a4i, in1=h4r, op=ALU.mult), reads=[a3_t, hpt], writes=[cd_t])
                    P.op("dve", lambda e: e.tensor_tensor(out=cc[:], in0=cc[:], in1=cd[:], op=ALU.add), reads=[cc_t, cd_t], writes=[cc_t])
                    P.op("dve", lambda e: e.tensor_tensor(out=br[:, :, 0:1], in0=br[:, :, 0:1], in1=ca[:], op=ALU.add), reads=[br_t, ca_t], writes=[br_t])
                    P.op("dve", lambda e: e.tensor_tensor(out=bi[:, :, 0:1], in0=bi[:, :, 0:1], in1=cc[:], op=ALU.add), reads=[bi_t, cc_t], writes=[bi_t])
                    (hr, hr_t), (hi, hi_t) = wk["hr"].next(), wk["hi"].next()
                    R4 = rhoT[:, sl4, :].rearrange("p a b -> p (a b)")
                    fl = lambda t: t[:].rearrange("p a b -> p (a b)")
                    P.op("dve", lambda e: e.tensor_tensor_scan(out=fl(hr), data0=R4, data1=fl(br), initial=0.0, op0=ALU.mult, op1=ALU.add),
                         reads=[tab_t, br_t], writes=[hr_t])
                    P.op("dve", lambda e: e.tensor_tensor_scan(out=fl(hi), data0=R4, data1=fl(bi), initial=0.0, op0=ALU.mult, op1=ALU.add),
                         reads=[tab_t, bi_t], writes=[hi_t])
                    cl_, sl_ = cosT[:, sl4, 127:128], sinT[:, sl4, 127:128]
                    (ca, ca_t), (cb, cb_t), (cc, cc_t), (cd, cd_t) = c4["a"].next(), c4["b"].next(), c4["c"].next(), c4["d"].next()
                    P.op("pool", lambda e: e.tensor_tensor(out=ca[:], in0=hr[:, :, 127:128], in1=cl_, op=ALU.mult), reads=[hr_t, tab_t], writes=[ca_t])
                    P.op("pool", lambda e: e.tensor_tensor(out=cb[:], in0=hi[:, :, 127:128], in1=sl_, op=ALU.mult), reads=[hi_t, tab_t], writes=[cb_t])
                    P.op("pool", lambda e: e.tensor_tensor(out=h4r, in0=ca[:], in1=cb[:], op=ALU.subtract), reads=[ca_t, cb_t], writes=[hpt])
                    P.op("pool", lambda e: e.tensor_tensor(out=cc[:], in0=hr[:, :, 127:128], in1=sl_, op=ALU.mult), reads=[hr_t, tab_t], writes=[cc_t])
                    P.op("pool", lambda e: e.tensor_tensor(out=cd[:], in0=hi[:, :, 127:128], in1=cl_, op=ALU.mult), reads=[hi_t, tab_t], writes=[cd_t])
                    P.op("pool", lambda e: e.tensor_tensor(out=h4i, in0=cc[:], in1=cd[:], op=ALU.add), reads=[cc_t, cd_t], writes=[hpt])
                    (u1, u1_t), (u2, u2_t) = wk["br"].next(), wk["bi"].next()
                    (hrb, hrb_t), (hib, hib_t) = hb["r"].next(), hb["i"].next()
                    C4, S4 = cosT[:, sl4, :], sinT[:, sl4, :]
                    P.op("pool", lambda e: e.tensor_tensor(out=u1[:], in0=hr[:], in1=C4, op=ALU.mult), reads=[hr_t, tab_t], writes=[u1_t])
                    P.op("pool", lambda e: e.tensor_tensor(out=u2[:], in0=hi[:], in1=S4, op=ALU.mult), reads=[hi_t, tab_t], writes=[u2_t])
                    P.op("pool", lambda e: e.tensor_tensor(out=hrb[:], in0=u1[:], in1=u2[:], op=ALU.subtract), reads=[u1_t, u2_t], writes=[hrb_t])
                    P.op("pool", lambda e: e.tensor_tensor(out=u1[:], in0=hr[:], in1=S4, op=ALU.mult), reads=[hr_t, tab_t, hrb_t], writes=[u1_t])
                    P.op("pool", lambda e: e.tensor_tensor(out=u2[:], in0=hi[:], in1=C4, op=ALU.mult), reads=[hi_t, tab_t, hrb_t], writes=[u2_t])
                    P.op("dve", lambda e: e.scalar_tensor_tensor(out=hib[:], in0=u1[:], scalar=-1.0, in1=u2[:], op0=ALU.mult, op1=ALU.subtract),
                         reads=[u1_t, u2_t], writes=[hib_t])
                    for pit in range(4):
                        i = ct * 4 + pit
                        P.op("pe", lambda e: e.matmul(py[cl][:, n * 128:(n + 1) * 128], Cb[:, 0, i, :], hrb[:, pit, :], start=(pit == 0), stop=False),
                             reads=[bc_t, hrb_t], writes=[py_t[cl]])
                        P.op("pe", lambda e: e.matmul(py[cl][:, n * 128:(n + 1) * 128], Cb[:, 1, i, :], hib[:, pit, :], start=False, stop=(pit == 3)),
                             reads=[bc_t, hib_t], writes=[py_t[cl]])
            for cl in range(4):
                ct = cth * 4 + cl
                ut, ut_t, ubt, ubt_t = cur[ct]
                (y2, y2_t), (w_, w_t) = ep["y"].next(), ep["w"].next()
                P.op("dve", lambda e: e.scalar_tensor_tensor(out=y2[:], in0=ut[:], scalar=dv[:, ct:ct + 1], in1=py[cl][:], op0=ALU.mult, op1=ALU.add),
                     reads=[ut_t, dv_t, py_t[cl]], writes=[y2_t])
                P.op("pool", lambda e: e.tensor_tensor(out=w_[:], in0=y2[:], in1=y2[:], op=ALU.mult), reads=[y2_t], writes=[w_t])
                P.op("pool", lambda e: e.tensor_scalar(out=w_[:], in0=w_[:], scalar1=0.044715, scalar2=1.0, op0=ALU.mult, op1=ALU.add), reads=[w_t], writes=[w_t])
                P.op("pool", lambda e: e.tensor_tensor(out=w_[:], in0=w_[:], in1=y2[:], op=ALU.mult), reads=[w_t, y2_t], writes=[w_t])
                P.op("act", lambda e: e.activation(out=w_[:], in_=w_[:], func=AF.Sigmoid, scale=1.5957691216057308), reads=[w_t], writes=[w_t])
                ot, ot_t, osem = ost.next()
                P.op("dve", lambda e: e.tensor_tensor(out=ot[:], in0=y2[:], in1=w_[:], op=ALU.mult), reads=[y2_t, w_t], writes=[ot_t])
                P.dma("sp", zT[ct * 128:(ct + 1) * 128, sc * 512:(sc + 1) * 512], ot[:], osem, reads=[ot_t])
    return P.finish(ost.trk)


def s5_layout(lam_re, lam_im, log_dt, b_re, b_im, c_re, c_im, gsl):
    G = 64
    lr = lam_re[gsl].reshape(32, 128).T
    li = lam_im[gsl].reshape(32, 128).T
    ld = np.repeat(log_dt[gsl].reshape(32, 2, 1), 64, axis=2).reshape(32, 128).T
    prm = np.ascontiguousarray(np.stack([lr, li, ld], axis=1)).astype(np.float32)
    Bt = np.zeros((2, 32, 128, 128), np.float32)
    Ct = np.zeros((2, 32, 128, 128), np.float32)
    for gi in range(G):
        g = gsl.start + gi
        pair, g2 = gi // 2, gi % 2
        pit = pair % 4
        ks = slice(pit * 32 + g2 * 16, pit * 32 + g2 * 16 + 16)
        ms = slice(g2 * 64, g2 * 64 + 64)
        Bt[0, pair, ks, ms] = b_re[g].T
        Bt[1, pair, ks, ms] = b_im[g].T
        Ct[0, pair, ms, ks] = c_re[g].T
        Ct[1, pair, ms, ks] = c_im[g].T
    return prm, Bt, Ct


CW = 2048


def build_cast(R):
    P = Prog()
    wi_ = P.dram("wflat", [R, 128, CW], F32, "ExternalInput")
    wo_ = P.dram("wb", [R, 128, CW], BF16, "ExternalOutput")
    ins = Slots(P, 3, [128, CW], F32, "cin")
    outs = Slots(P, 3, [128, CW], BF16, "cout")
    for r in range(R):
        it, it_t, isem = ins.next()
        P.dma("sp", it[:], wi_[r], isem, writes=[it_t])
        ot, ot_t, osem = outs.next()
        eng = ("act", "dve", "pool")[r % 3]
        if eng == "act":
            P.op("act", lambda e: e.activation(out=ot[:], in_=it[:], func=AF.Copy), reads=[it_t], writes=[ot_t])
        else:
            P.op(eng, lambda e: e.tensor_copy(out=ot[:], in_=it[:]), reads=[it_t], writes=[ot_t])
        P.dma("sp", wo_[r], ot[:], osem, reads=[ot_t])
    return P.finish(outs.trk)


def build_ada(NL, NCOL):
    P = Prog()
    NBK = NCOL // 512
    cT = P.dram("cT", [128, DC, B], F32, "ExternalInput")
    w = P.dram("w", [NL, D, NCOL], F32, "ExternalInput")
    bias = P.dram("bias", [NL, B, NCOL], F32, "ExternalInput")
    mod = P.dram("mod", [NL, B, NCOL], F32, "ExternalOutput")
    s0 = P.dsem()
    ct = P.sb([128, DC, B], F32, "cts"); ct_t = Trk()
    P.dma("sp", ct[:], cT, s0, writes=[ct_t])
    cond = P.sb([128, DC, B], F32, "cond"); cond_t = Trk()
    P.op("act", lambda e: e.activation(out=cond[:], in_=ct[:], func=AF.Silu), reads=[ct_t], writes=[cond_t])
    ws = Slots(P, 4, [128, NCOL], F32, "aw")
    bs = Slots(P, 2, [B, NCOL], F32, "ab")
    os_ = Slots(P, 2, [B, NCOL], F32, "ao")
    ps = [P.ps([128, 512], F32, name=f"aps{i}") for i in range(NBK)]
    ps_t = [Trk() for _ in range(NBK)]
    for l in range(NL):
        bt, bt_t, bsem = bs.next()
        P.dma("sp", bt[:], bias[l], bsem, writes=[bt_t])
        for k in range(DC):
            wt, wt_t, wsem = ws.next()
            P.dma("sp", wt[:], w[l, k * 128:(k + 1) * 128, :], wsem, writes=[wt_t])
            for nb in range(NBK):
                P.op("pe", lambda e: e.matmul(ps[nb][0:B, :], cond[:, k, :], wt[:, nb * 512:(nb + 1) * 512], start=(k == 0), stop=(k == DC - 1)),
                     reads=[cond_t, wt_t], writes=[ps_t[nb]])
        ot, ot_t, osem = os_.next()
        for nb in range(NBK):
            P.op("dve", lambda e: e.tensor_tensor(out=ot[:, nb * 512:(nb + 1) * 512], in0=ps[nb][0:B, :], in1=bt[:, nb * 512:(nb + 1) * 512], op=ALU.add),
                 reads=[ps_t[nb], bt_t], writes=[ot_t])
        P.dma("sp", mod[l], ot[:], osem, reads=[ot_t])
    return P.finish(os_.trk)


_CACHE = {}


def _prog(key, fn):
    if key not in _CACHE:
        _CACHE[key] = fn()
    return _CACHE[key]


def _run(nc, in_maps):
    res = run_bass_kernel_spmd(nc, in_maps, core_ids=list(range(NCORES)))
    return res.results


def _c(a):
    return np.ascontiguousarray(a)


def kernel(x, c, positions, ln1_g, ln2_g, ada_w, ada_b, mlp_w1, mlp_w2,
           sb_w_in, sb_q_gain, sb_k_gain, sb_w_out,
           s5_w_in, s5_lambda_re, s5_lambda_im, s5_log_dt, s5_b_re, s5_b_im,
           s5_c_re, s5_c_im, s5_d, s5_w_glu,
           dsa_w_in, dsa_q_gain, dsa_k_gain, dsa_w_out):
    f32 = lambda a: np.asarray(a, dtype=np.float32)
    x, c = f32(x), f32(c)
    positions = np.asarray(positions).astype(np.int32)
    NL = ln1_g.shape[0]
    TH = S // 2
    KB = S // 128

    names = ["mlp_w1", "mlp_w2", "sb_w_in", "sb_w_out", "s5_w_in", "s5_w_glu", "dsa_w_in", "dsa_w_out"]
    arrs = [f32(a) for a in (mlp_w1, mlp_w2, sb_w_in, sb_w_out, s5_w_in, s5_w_glu, dsa_w_in, dsa_w_out)]
    sizes = [a.size for a in arrs]
    tot = sum(sizes)
    per = 128 * CW
    R = -(-tot // (per * NCORES))
    flat = np.zeros(R * per * NCORES, np.float32)
    o = 0
    for a in arrs:
        flat[o:o + a.size] = a.reshape(-1)
        o += a.size
    flat = flat.reshape(NCORES, R, 128, CW)
    nc = _prog(("cast", R), lambda: build_cast(R))
    res = _run(nc, [{"wflat": flat[i]} for i in range(NCORES)])
    fb = np.concatenate([r["wb"].reshape(-1) for r in res])
    del flat
    wb = {}
    o = 0
    for n, a in zip(names, arrs):
        wb[n] = fb[o:o + a.size].reshape(a.shape)
        o += a.size
    del fb, arrs

    NCOL = 6 * D // NCORES
    cT = _c(c.T.reshape(DC, 128, B).transpose(1, 0, 2))
    ada_w, ada_b = f32(ada_w), f32(ada_b)
    nc = _prog(("ada", NL), lambda: build_ada(NL, NCOL))
    ims = []
    for i in range(NCORES):
        sl = slice(i * NCOL, (i + 1) * NCOL)
        ims.append({"cT": cT, "w": _c(ada_w[:, :, sl]), "bias": _c(np.broadcast_to(ada_b[:, None, sl], (NL, B, NCOL)))})
    res = _run(nc, ims)
    mod = np.concatenate([r["mod"] for r in res], axis=2)
    del ims

    xT = _c(x.transpose(0, 2, 1))
    idn = np.eye(128, dtype=np.float32).astype(NPBF)
    fv = rope_consts()
    cores = [(b, j) for b in range(B) for j in range(2)]
    cnt = [0, 0, 0]
    for l in range(NL):
        kind = l % 3
        jm = cnt[kind]
        cnt[kind] += 1
        sh1, sc1, g1, sh2, sc2, g2 = [mod[l][:, i * D:(i + 1) * D] for i in range(6)]
        if kind == 0:
            wf, items, NOUT, wt, gv = proj_plan_sb(wb["sb_w_in"][jm], f32(sb_q_gain[jm]), f32(sb_k_gain[jm]))
            odt, rope_on = BF16, False
        elif kind == 1:
            wf, items, NOUT, wt, gv = proj_plan_s5(wb["s5_w_in"][jm])
            odt, rope_on = F32, False
        else:
            wf, items, NOUT, wt, gv = proj_plan_dsa(wb["dsa_w_in"][jm], f32(dsa_q_gain[jm]), f32(dsa_k_gain[jm]))
            odt, rope_on = BF16, True
        NT = 0 if wt is None else wt.shape[0]
        nc = _prog(("proj", kind), lambda: build_proj(TH, items, wf.shape[0], NOUT, odt, NT, gv.shape[1], rope_on))
        ims = []
        for (b, hf) in cores:
            im = {"xT": _c(xT[b][:, hf * TH:(hf + 1) * TH]),
                  "vec": _c(np.stack([lay_vec(f32(ln1_g[l])), lay_vec(sc1[b]), lay_vec(sh1[b])], axis=1)),
                  "wf": wf, "gv": gv}
            if NT:
                im["wt"] = wt
            if rope_on:
                im["pos"] = _c(positions[b][None, hf * TH:(hf + 1) * TH])
                im["fv"] = fv
            ims.append(im)
        res = _run(nc, ims)
        yT = [np.concatenate([res[2 * b]["yT"], res[2 * b + 1]["yT"]], axis=1) for b in range(B)]
        if NT:
            vo = [np.concatenate([res[2 * b]["vo"], res[2 * b + 1]["vo"]], axis=0) for b in range(B)]
        del ims, res
        if kind in (0, 2):
            ims = []
            for (b, j) in cores:
                hs = slice(8 * j, 8 * j + 8)
                y3 = yT[b].reshape(NOUT, 128, S)
                v4 = vo[b].reshape(KB, 128, NH, 128)[:, :, hs, :].transpose(2, 1, 0, 3)
                im = {"qT": _c(y3[0:NH][hs]), "kT": _c(y3[NH:2 * NH][hs]), "v": _c(v4)}
                if kind == 2:
                    im["qiT"] = _c(y3[2 * NH:2 * NH + 8])
                    im["kiT"] = _c(y3[2 * NH + 8])
                    im["wi"] = _c(y3[2 * NH + 9][:16].reshape(16, KB, 128).transpose(2, 1, 0))
                    im["idn"] = idn
                ims.append(im)
            if kind == 0:
                nc = _prog(("sb",), lambda: build_sb(S, 8))
            else:
                nc = _prog(("dsa",), lambda: build_dsa(S, 8, min(256, S // 4)))
            res = _run(nc, ims)
            oT = [np.concatenate([res[2 * b]["oT"], res[2 * b + 1]["oT"]], axis=0) for b in range(B)]
            wo = lay_w(wb["sb_w_out"][jm] if kind == 0 else wb["dsa_w_out"][jm])
            glu = False
        else:
            ims = []
            lre, lim, ldt = f32(s5_lambda_re[jm]), f32(s5_lambda_im[jm]), f32(s5_log_dt[jm])
            bre, bim, cre, cim = f32(s5_b_re[jm]), f32(s5_b_im[jm]), f32(s5_c_re[jm]), f32(s5_c_im[jm])
            dsk = f32(s5_d[jm])
            tl = _c(np.tile(np.arange(128, dtype=np.float32)[None, :], (128, 1)))
            lay = [s5_layout(lre, lim, ldt, bre, bim, cre, cim, slice(64 * j, 64 * j + 64)) for j in range(2)]
            for (b, j) in cores:
                prm, Bt, Ct = lay[j]
                ims.append({"uT": _c(yT[b][j * 1024:(j + 1) * 1024]), "prm": prm, "Bt": Bt, "Ct": Ct,
                            "dv": lay_vec(dsk[j * 1024:(j + 1) * 1024]), "tl": tl})
            nc = _prog(("s5",), lambda: build_s5(S))
            res = _run(nc, ims)
            oT = [np.concatenate([res[2 * b]["zT"], res[2 * b + 1]["zT"]], axis=0) for b in range(B)]
            wo = lay_w(wb["s5_w_glu"][jm])
            glu = True
        del ims, res, yT
        w1l, w2l = lay_w(wb["mlp_w1"][l]), lay_w(wb["mlp_w2"][l])
        nc = _prog(("mlp", glu), lambda: build_mlp(TH, glu))
        ims = []
        for (b, hf) in cores:
            ts = slice(hf * TH, (hf + 1) * TH)
            vec = np.stack([lay_vec(g1[b]), lay_vec(f32(ln2_g[l])), lay_vec(sc2[b]), lay_vec(sh2[b]), lay_vec(g2[b])], axis=1)
            ims.append({"xT": _c(xT[b][:, ts]), "oT": _c(oT[b][:, ts]), "wo": wo, "w1": w1l, "w2": w2l, "vec": _c(vec)})
        res = _run(nc, ims)
        xT = np.stack([np.concatenate([res[2 * b]["yT"], res[2 * b + 1]["yT"]], axis=1) for b in range(B)])
        del ims, res, oT
    return _c(xT.transpose(0, 2, 1)).astype(np.float32)
```
